# Optimizing a Trainium2 kernel written in Bass

```python
import math
import jax, jax.numpy as jnp
from jax import lax
import numpy as np

D_MODEL = 1024
BATCH = 8
SEQ = 2048
DEPTH = 4

HEAD_DIM = 64
DIFF_HEADS = 4
DIFF_VDIM = 2 * HEAD_DIM
SB_HEADS = 8
DIFF_WIDTH = DIFF_HEADS * DIFF_VDIM
SB_WIDTH = SB_HEADS * HEAD_DIM
MIX_WIDTH = DIFF_WIDTH + SB_WIDTH
DIFF_QK = DIFF_HEADS * 2 * HEAD_DIM
SB_QK = SB_HEADS * HEAD_DIM
IN_WIDTH = 2 * DIFF_QK + DIFF_WIDTH + 2 * SB_QK + SB_WIDTH
Q_BLOCK = 128

PEER_HEADS = 8
N_KEYS = 128
N_EXPERTS = N_KEYS * N_KEYS
PEER_TOPK = 16
HALF_Q = 128
QUERY_DIM = 2 * HALF_Q
TOKEN_CHUNK = 128

DEEPNORM_ALPHA = (2.0 * DEPTH) ** 0.25
DEEPNORM_BETA = (8.0 * DEPTH) ** -0.25
LN_EPS = 1e-5
RMS_EPS = 1e-5

kernel_name = "hymba_diff_stickbreak_peer_deepnorm"


def layer_norm(h, g, b):
    hf = h.astype(jnp.float32)
    mu = jnp.mean(hf, axis=-1, keepdims=True)
    var = jnp.mean(jnp.square(hf - mu), axis=-1, keepdims=True)
    y = (hf - mu) * lax.rsqrt(var + LN_EPS) * g.astype(jnp.float32) + b.astype(jnp.float32)
    return y.astype(h.dtype)


def alibi_slopes():
    return jnp.asarray([2.0 ** (-8.0 * (i + 1) / DIFF_HEADS) for i in range(DIFF_HEADS)], jnp.float32)


def token_mixer(h, w_in, lq1, lk1, lq2, lk2, subln_g, w_o, layer):
    B, S, _ = h.shape
    proj = h @ w_in
    s1 = DIFF_QK
    s2 = s1 + DIFF_QK
    s3 = s2 + DIFF_WIDTH
    s4 = s3 + SB_QK
    s5 = s4 + SB_QK
    dq, dk, dv, sq, sk, sv = jnp.split(proj, [s1, s2, s3, s4, s5], axis=-1)
    dq = dq.reshape(B, S, DIFF_HEADS, 2, HEAD_DIM).transpose(0, 2, 3, 1, 4)
    dk = dk.reshape(B, S, DIFF_HEADS, 2, HEAD_DIM).transpose(0, 2, 3, 1, 4)
    dv = dv.reshape(B, S, DIFF_HEADS, DIFF_VDIM).transpose(0, 2, 1, 3)
    sq = sq.reshape(B, S, SB_HEADS, HEAD_DIM).transpose(0, 2, 1, 3)
    sk = sk.reshape(B, S, SB_HEADS, HEAD_DIM).transpose(0, 2, 1, 3)
    sv = sv.reshape(B, S, SB_HEADS, HEAD_DIM).transpose(0, 2, 1, 3)

    lambda_init = 0.8 - 0.6 * math.exp(-0.3 * layer)
    lam = (jnp.exp(jnp.sum(lq1.astype(jnp.float32) * lk1.astype(jnp.float32)))
           - jnp.exp(jnp.sum(lq2.astype(jnp.float32) * lk2.astype(jnp.float32)))
           + lambda_init)
    slopes = alibi_slopes()[None, :, None, None, None]
    scale = HEAD_DIM ** -0.5

    diff_out, sb_out = [], []
    for blk in range(S // Q_BLOCK):
        q0 = blk * Q_BLOCK
        kv_len = q0 + Q_BLOCK
        t_pos = q0 + jnp.arange(Q_BLOCK)
        s_pos = jnp.arange(kv_len)
        dist = (t_pos[:, None] - s_pos[None, :]).astype(jnp.float32)

        sc = jnp.einsum('bhmqd,bhmkd->bhmqk', dq[:, :, :, q0:kv_len], dk[:, :, :, :kv_len]).astype(jnp.float32) * scale
        sc = jnp.where(dist >= 0, sc - slopes * dist, -jnp.inf)
        p = jax.nn.softmax(sc, axis=-1)
        attn = p[:, :, 0] - lam * p[:, :, 1]
        diff_out.append(jnp.einsum('bhqk,bhkd->bhqd', attn.astype(dv.dtype), dv[:, :, :kv_len]))

        z = jnp.einsum('bgqd,bgkd->bgqk', sq[:, :, q0:kv_len], sk[:, :, :kv_len]).astype(jnp.float32) * scale
        strict = dist > 0
        log_fail = jnp.where(strict, jax.nn.log_sigmoid(-z), 0.0)
        log_later = lax.cumsum(log_fail, axis=3, reverse=True) - log_fail
        w = jnp.where(strict, jnp.exp(jax.nn.log_sigmoid(z) + log_later), 0.0)
        sb_out.append(jnp.einsum('bgqk,bgkd->bgqd', w.astype(sv.dtype), sv[:, :, :kv_len]))

    diff = jnp.concatenate(diff_out, axis=2).astype(jnp.float32)
    diff = diff * lax.rsqrt(jnp.mean(jnp.square(diff), axis=-1, keepdims=True) + RMS_EPS)
    diff = diff * subln_g.astype(jnp.float32) * (1.0 - lambda_init)
    diff = diff.astype(h.dtype).transpose(0, 2, 1, 3).reshape(B, S, DIFF_WIDTH)
    sb = jnp.concatenate(sb_out, axis=2).transpose(0, 2, 1, 3).reshape(B, S, SB_WIDTH)
    mixed = jnp.concatenate([diff, sb.astype(h.dtype)], axis=-1)
    return mixed @ w_o


def peer_ffn(h, w_query, sub_keys, expert_u, expert_v):
    B, S, D = h.shape
    T = B * S
    xt = h.reshape(T, D)
    q = (xt @ w_query).reshape(T, PEER_HEADS, 2, HALF_Q).astype(jnp.float32)
    scores = jnp.einsum('thpc,hpnc->thpn', q, sub_keys.astype(jnp.float32))
    top_s, top_i = lax.top_k(scores, PEER_TOPK)
    cand_s = top_s[:, :, 0, :, None] + top_s[:, :, 1, None, :]
    cand_i = top_i[:, :, 0, :, None] * N_KEYS + top_i[:, :, 1, None, :]
    cand_s = cand_s.reshape(T, PEER_HEADS, PEER_TOPK * PEER_TOPK)
    cand_i = cand_i.reshape(T, PEER_HEADS, PEER_TOPK * PEER_TOPK)
    best_s, best_pos = lax.top_k(cand_s, PEER_TOPK)
    idx = jnp.take_along_axis(cand_i, best_pos, axis=-1)
    gate = jax.nn.softmax(best_s, axis=-1)
    n_sel = PEER_HEADS * PEER_TOPK
    n_chunks = T // TOKEN_CHUNK

    def expert_chunk(args):
        xc, ic, gc = args
        u = expert_u[ic]
        act = jax.nn.gelu(jnp.einsum('cd,ced->ce', xc, u).astype(jnp.float32), approximate=False)
        v = expert_v[ic]
        return jnp.einsum('ce,ced->cd', (gc * act).astype(v.dtype), v)

    y = lax.map(expert_chunk, (xt.reshape(n_chunks, TOKEN_CHUNK, D),
                               idx.reshape(n_chunks, TOKEN_CHUNK, n_sel),
                               gate.reshape(n_chunks, TOKEN_CHUNK, n_sel)))
    return y.reshape(B, S, D).astype(h.dtype)


def setup_inputs(seed: int = 0) -> dict:
    key = jax.random.key(seed)
    ks = jax.random.split(key, 16)
    f32 = jnp.float32
    beta = DEEPNORM_BETA
    col_scale = np.concatenate([np.ones(2 * DIFF_QK), np.full(DIFF_WIDTH, beta),
                                np.ones(2 * SB_QK), np.full(SB_WIDTH, beta)]).astype(np.float32)
    x = jax.random.normal(ks[0], (BATCH, SEQ, D_MODEL), f32)
    w_in = jax.random.normal(ks[1], (DEPTH, D_MODEL, IN_WIDTH), f32) * (D_MODEL ** -0.5) * jnp.asarray(col_scale)
    lam_q1 = 0.1 * jax.random.normal(ks[2], (DEPTH, HEAD_DIM), f32)
    lam_k1 = 0.1 * jax.random.normal(ks[3], (DEPTH, HEAD_DIM), f32)
    lam_q2 = 0.1 * jax.random.normal(ks[4], (DEPTH, HEAD_DIM), f32)
    lam_k2 = 0.1 * jax.random.normal(ks[5], (DEPTH, HEAD_DIM), f32)
    subln_g = 1.0 + 0.02 * jax.random.normal(ks[6], (DEPTH, DIFF_VDIM), f32)
    w_o = jax.random.normal(ks[7], (DEPTH, MIX_WIDTH, D_MODEL), f32) * (MIX_WIDTH ** -0.5) * beta
    ln1_g = 1.0 + 0.02 * jax.random.normal(ks[8], (DEPTH, D_MODEL), f32)
    ln1_b = 0.02 * jax.random.normal(ks[9], (DEPTH, D_MODEL), f32)
    w_query = jax.random.normal(ks[10], (DEPTH, D_MODEL, PEER_HEADS * QUERY_DIM), f32) * (D_MODEL ** -0.5)
    sub_keys = jax.random.normal(ks[11], (DEPTH, PEER_HEADS, 2, N_KEYS, HALF_Q), f32) * (HALF_Q ** -0.5)
    expert_u = jax.random.normal(ks[12], (DEPTH, N_EXPERTS, D_MODEL), f32) * (D_MODEL ** -0.5) * beta
    expert_v = jax.random.normal(ks[13], (DEPTH, N_EXPERTS, D_MODEL), f32) * beta
    ln2_g = 1.0 + 0.02 * jax.random.normal(ks[14], (DEPTH, D_MODEL), f32)
    ln2_b = 0.02 * jax.random.normal(ks[15], (DEPTH, D_MODEL), f32)
    return {"x": x, "w_in": w_in, "lam_q1": lam_q1, "lam_k1": lam_k1, "lam_q2": lam_q2,
            "lam_k2": lam_k2, "subln_g": subln_g, "w_o": w_o, "ln1_g": ln1_g, "ln1_b": ln1_b,
            "w_query": w_query, "sub_keys": sub_keys, "expert_u": expert_u, "expert_v": expert_v,
            "ln2_g": ln2_g, "ln2_b": ln2_b}


def reference(x, w_in, lam_q1, lam_k1, lam_q2, lam_k2, subln_g, w_o, ln1_g, ln1_b,
              w_query, sub_keys, expert_u, expert_v, ln2_g, ln2_b):
    h = x
    for l in range(DEPTH):
        mix = token_mixer(h, w_in[l], lam_q1[l], lam_k1[l], lam_q2[l], lam_k2[l], subln_g[l], w_o[l], l)
        h = layer_norm(DEEPNORM_ALPHA * h + mix, ln1_g[l], ln1_b[l])
        ffn = peer_ffn(h, w_query[l], sub_keys[l], expert_u[l], expert_v[l])
        h = layer_norm(DEEPNORM_ALPHA * h + ffn, ln2_g[l], ln2_b[l])
    return h
```

```python
import math
import numpy as np
from contextlib import ExitStack
import ml_dtypes
import concourse.bass as bass
import concourse.mybir as mybir
from concourse.bass_utils import run_bass_kernel_spmd

F32 = mybir.dt.float32; BF16 = mybir.dt.bfloat16; I32 = mybir.dt.int32; U32 = mybir.dt.uint32; U16 = mybir.dt.uint16
AF = mybir.ActivationFunctionType; ALU = mybir.AluOpType; AX = mybir.AxisListType

D = 1024; DEPTH = 4; NEXP = 16384
ALPHA = (2.0 * DEPTH) ** 0.25
LN_EPS = 1e-5; RMS_EPS = 1e-5
SCALE = 0.125
NB_G = 16


class Buf:
    def __init__(self, name, t=None):
        self.name = name; self.t = t; self.w = None; self.r = {}

    def __getitem__(self, k):
        return self.t[k]


class DSem:
    def __init__(self, sem):
        self.sem = sem; self.count = 0


class Eng:
    def __init__(self, name, eng, sem):
        self.name = name; self.eng = eng; self.sem = sem; self.count = 0; self.seen = {}


class KB:
    def __init__(self, nc, es):
        self.nc = nc; self.es = es
        self.E = {}
        for n, e in (("pe", nc.tensor), ("act", nc.scalar), ("dve", nc.vector), ("pool", nc.gpsimd), ("sp", nc.sync)):
            self.E[n] = Eng(n, e, es.enter_context(nc.semaphore("sem_" + n)))
        self.dsems = {}
        self.semobj = {}
        for E in self.E.values():
            self.semobj[id(E.sem)] = (E.sem, E)
        self.uid = 0

    def dsem(self, name):
        if name not in self.dsems:
            d = DSem(self.es.enter_context(self.nc.semaphore("ds_" + name)))
            self.dsems[name] = d; self.semobj[id(d.sem)] = (d.sem, d)
        return self.dsems[name]

    def sb(self, name, shape, dtype, es=None):
        self.uid += 1
        t = (es or self.es).enter_context(self.nc.sbuf_tensor(f"{name}_{self.uid}", list(shape), dtype))
        return Buf(name, t)

    def ps(self, name, shape, dtype, es=None):
        self.uid += 1
        t = (es or self.es).enter_context(self.nc.psum_tensor(f"{name}_{self.uid}", list(shape), dtype))
        return Buf(name, t)

    def _wait(self, E, toks, strict=()):
        best = {}
        for (sid, v) in list(toks) + list(strict):
            if best.get(sid, 0) < v:
                best[sid] = v
        sown = 0
        for (sid, v) in strict:
            if self.semobj[sid][1] is E:
                sown = max(sown, v)
        for sid, v in best.items():
            sem, owner = self.semobj[sid]
            if isinstance(owner, DSem):
                v = owner.count
            if owner is E and E.name == "pe":
                continue
            if E.seen.get(sid, 0) < v:
                E.eng.wait_ge(sem, v); E.seen[sid] = v

    def _deps(self, reads, writes):
        toks = []
        for b in reads:
            if b.w:
                toks.append(b.w)
        for b in writes:
            if b.w:
                toks.append(b.w)
            toks.extend(b.r.items())
        return toks

    def _mark(self, tok, reads, writes):
        sid, v = tok
        for b in reads:
            if b.r.get(sid, 0) < v:
                b.r[sid] = v
        for b in writes:
            b.w = tok; b.r = {}

    def op(self, en, fn, reads=(), writes=(), sreads=()):
        E = self.E[en]
        self._wait(E, self._deps(reads, writes), [b.w for b in sreads if b.w])
        ins = fn(E.eng)
        E.count += 1
        ins.then_inc(E.sem, 1)
        self._mark((id(E.sem), E.count), list(reads) + list(sreads), writes)
        return ins

    def dma(self, qn, dname, out, in_, reads=(), writes=(), indirect=None):
        E = self.E[qn]; d = self.dsem(dname)
        self._wait(E, self._deps(reads, writes))
        if indirect is not None:
            ins = E.eng.indirect_dma_start(out=out, out_offset=None, in_=in_,
                                           in_offset=bass.IndirectOffsetOnAxis(ap=indirect, axis=0))
        else:
            ins = E.eng.dma_start(out=out, in_=in_)
        d.count += 16
        ins.then_inc(d.sem, 16)
        self._mark((id(d.sem), d.count), reads, writes)
        return ins

    def barrier(self):
        toks = [(id(E.sem), E.count) for E in self.E.values() if E.count] + \
               [(id(d.sem), d.count) for d in self.dsems.values() if d.count]
        for E in self.E.values():
            self._wait(E, toks)


def bc(ap, shape):
    return ap.to_broadcast(list(shape))


def host_consts(S):
    c = {}
    c["ident"] = np.eye(128, dtype=np.float32)
    j = np.arange(128)
    c["tri"] = (j[:, None] >= j[None, :]).astype(np.float32)
    c["ones"] = np.ones((128, 128), np.float32)
    q = np.arange(512)
    mc = np.zeros((128, 4, 512), np.float32); ms = np.zeros((128, 4, 512), np.float32)
    for jj in range(4):
        mc[:, jj, :] = (q[None, :] >= jj * 128 + j[:, None])
        ms[:, jj, :] = (q[None, :] > jj * 128 + j[:, None])
    c["maskc"] = mc.astype(ml_dtypes.bfloat16); c["masks"] = ms.astype(ml_dtypes.bfloat16)
    pos = np.arange(S)
    hi = (pos // 128) * 128.0; lo = (pos % 128) * 1.0
    posk = np.zeros((4, S), np.float32); posk[0] = hi; posk[1] = lo; posk[2] = 1; posk[3] = 1
    c["posk"] = posk.astype(ml_dtypes.bfloat16)
    posq = np.zeros((4, 4, S), np.float32)
    for hh in range(4):
        cc = 2.0 ** (-8.0 * (hh + 1) / 4) / SCALE
        posq[0, hh] = cc; posq[1, hh] = cc; posq[2, hh] = -cc * hi; posq[3, hh] = -cc * lo
    c["posq"] = posq.astype(ml_dtypes.bfloat16)
    c["iota16"] = np.tile(np.arange(16, dtype=np.float32)[None, :], (128, 1))
    c["zeros"] = np.zeros((128, 512), ml_dtypes.bfloat16)
    return c


def build_program(S, L, do_attn=True, do_peer=True, dbg=False, stop=None):
    NT = S // 128
    NQB = S // 512 if S >= 512 else 1
    QBW = min(S, 512)
    TPB = QBW // 128
    nc = bass.Bass("TRN2", target_bir_lowering=False)
    dt = nc.dram_tensor
    x_h = dt("x", [S, D], F32, kind="ExternalInput")
    w_in_h = dt("w_in", [L, D, 3072], F32, kind="ExternalInput")
    lam_h = dt("lam4", [L, 4, 64], F32, kind="ExternalInput")
    subg_h = dt("subln_g", [L, 128], F32, kind="ExternalInput")
    w_o_h = dt("w_o", [L, D, D], F32, kind="ExternalInput")
    lng_h = dt("ln_gb", [L, 4, D], F32, kind="ExternalInput")
    wq_h = dt("w_query", [L, D, 2048], F32, kind="ExternalInput")
    keys_h = dt("sub_keys", [L, 16, 128, 128], F32, kind="ExternalInput")
    eu_h = dt("expert_u", [L * NEXP, D], F32, kind="ExternalInput")
    ev_h = dt("expert_v", [L * NEXP, D], F32, kind="ExternalInput")
    c_ident = dt("c_ident", [128, 128], F32, kind="ExternalInput")
    c_tri = dt("c_tri", [128, 128], F32, kind="ExternalInput")
    c_ones = dt("c_ones", [128, 128], F32, kind="ExternalInput")
    c_maskc = dt("c_maskc", [128, 4, 512], BF16, kind="ExternalInput")
    c_masks = dt("c_masks", [128, 4, 512], BF16, kind="ExternalInput")
    c_posk = dt("c_posk", [4, S], BF16, kind="ExternalInput")
    c_posq = dt("c_posq", [4, 4, S], BF16, kind="ExternalInput")
    c_iota = dt("c_iota16", [128, 16], F32, kind="ExternalInput")
    c_zeros = dt("c_zeros", [128, 512], BF16, kind="ExternalInput")
    out_h = dt("out", [S, D], F32, kind="ExternalOutput")
    uv_h = dt("uv_scr", [L * NEXP, 2048], BF16, kind="Internal")
    dbg_h = {}
    if dbg:
        dbg_h["h1"] = dt("dbg_h1", [S, D], F32, kind="ExternalOutput")

    with ExitStack() as es:
        kb = KB(nc, es)
        op = kb.op

        def dump(name, ap, shape, dtype, reads):
            if not dbg or name in dbg_h:
                return
            dbg_h[name] = dt("dbg_" + name, list(shape), dtype, kind="ExternalOutput")
            kb.dma("sp", "dbg", dbg_h[name].ap(), ap, reads=reads)
        h = kb.sb("h", [128, NT, D], F32)
        ident = kb.sb("ident", [128, 128], F32)
        tri = kb.sb("tri", [128, 128], F32)
        ones = kb.sb("ones", [128, 128], F32)
        onesb = kb.sb("onesb", [128, 128], BF16)
        zeros = kb.sb("zeros", [128, 512], BF16)
        iota16 = kb.sb("iota16", [128, 16], F32)
        pb = [kb.ps(f"pb{i}", [128, 512], F32) for i in range(8)]
        ht = [Buf(f"h{t}") for t in range(NT)]

        kb.dma("sp", "c0", ident.t[:], c_ident.ap(), writes=[ident])
        kb.dma("sp", "c0", tri.t[:], c_tri.ap(), writes=[tri])
        kb.dma("sp", "c0", ones.t[:], c_ones.ap(), writes=[ones])
        kb.dma("sp", "c0", zeros.t[:], c_zeros.ap(), writes=[zeros])
        kb.dma("sp", "c0", iota16.t[:], c_iota.ap(), writes=[iota16])
        xv = x_h.ap().rearrange("(n p) d -> p n d", p=128)
        for t in range(NT):
            kb.dma("sp", "xin", h.t[:, t, :], xv[:, t, :], writes=[ht[t]])
        op("pool", lambda e: e.tensor_copy(out=onesb.t[:], in_=ones.t[:]), reads=[ones], writes=[onesb])
        identb = kb.sb("identb", [128, 128], BF16)
        op("pool", lambda e: e.tensor_copy(out=identb.t[:], in_=ident.t[:]), reads=[ident], writes=[identb])

        def build_hT(hT):
            for t in range(NT):
                for half in range(2):
                    bank = pb[(2 * t + half) % 4]
                    for c4 in range(4):
                        c = half * 4 + c4
                        op("pe", lambda e: e.transpose(out=bank.t[:, c4 * 128:(c4 + 1) * 128], in_=h.t[:, t, c * 128:(c + 1) * 128],
                                                       identity=ident.t[:]), reads=[ht[t], ident], writes=[bank])
                    op("act", lambda e: e.activation(out=hT.t[:, half * 4:half * 4 + 4, t * 128:(t + 1) * 128],
                                                     in_=bank.t[:].rearrange("p (c n) -> p c n", c=4), func=AF.Copy),
                       reads=[bank], writes=[hT])

        def layer_norm_tile(t, src, gt, bt_, tmp, st, mv, r2):
            for c in range(2):
                op("dve", lambda e: e.bn_stats(out=st.t[:, c, :], in_=src[:, c * 512:(c + 1) * 512]), reads=[tmp, ht[t]], writes=[st])
            op("dve", lambda e: e.bn_aggr(out=mv.t[:], in_=st.t[:].rearrange("p a b -> p (a b)")), reads=[st], writes=[mv])
            op("act", lambda e: e.activation(out=r2.t[:, 0:1], in_=mv.t[:, 1:2], func=AF.Ln, bias=epsb.t[:, 0:1], scale=1.0), reads=[mv, epsb], writes=[r2])
            op("act", lambda e: e.activation(out=r2.t[:, 1:2], in_=r2.t[:, 0:1], func=AF.Exp, scale=-0.5), reads=[r2], writes=[r2])
            op("dve", lambda e: e.tensor_scalar(out=tmp.t[:], in0=src, scalar1=mv.t[:, 0:1], scalar2=r2.t[:, 1:2],
                                                op0=ALU.subtract, op1=ALU.mult), reads=[tmp, ht[t]], sreads=[mv, r2], writes=[tmp])
            op("dve", lambda e: e.tensor_tensor(out=tmp.t[:], in0=tmp.t[:], in1=gt.t[:], op=ALU.mult), reads=[tmp, gt], writes=[tmp])
            op("dve", lambda e: e.tensor_tensor(out=h.t[:, t, :], in0=tmp.t[:], in1=bt_.t[:], op=ALU.add), reads=[tmp, bt_], writes=[ht[t]])

        scr = [Buf(f"scr{i}") for i in range(L)]

        def convert_layer(cl):
            CH = 2048
            d = kb.dsem(f"cv{cl}")
            for c in range(NEXP // CH):
                r0 = cl * NEXP + c * CH
                if d.count >= 16 * 4:
                    kb.E["pool"].eng.wait_ge(d.sem, d.count - 16 * 3)
                kb.dma("pool", f"cv{cl}", uv_h[r0:r0 + CH, 0:D], eu_h[r0:r0 + CH, :], writes=[scr[cl]])
                kb.dma("pool", f"cv{cl}", uv_h[r0:r0 + CH, D:2 * D], ev_h[r0:r0 + CH, :], writes=[scr[cl]])

        epsb = kb.sb("epsb", [128, 1], F32)
        op("pool", lambda e: e.memset(epsb.t[:], LN_EPS), writes=[epsb])

        for l in range(L):
            li = 0.8 - 0.6 * math.exp(-0.3 * l)
            if do_attn:
                with ExitStack() as sa:
                    hT = kb.sb("hT", [128, 8, S], BF16, sa)
                    build_hT(hT)
                    wst = kb.sb("wst", [128, 8, 384], F32, sa)
                    wbf = kb.sb("wbf", [128, 8, 384], BF16, sa)
                    wost = kb.sb("wost", [128, D], F32, sa)
                    wobf = kb.sb("wobf", [128, D], BF16, sa)
                    QT = kb.sb("QT", [128, S], BF16, sa)
                    KT = kb.sb("KT", [128, S], BF16, sa)
                    V = kb.sb("V", [128, NT, 128], BF16, sa)
                    mixT = kb.sb("mixT", [128, S], BF16, sa)
                    maskc = kb.sb("maskc", [128, 4, 512], BF16, sa)
                    masks = kb.sb("masks", [128, 4, 512], BF16, sa)
                    posk = kb.sb("posk", [4, S], BF16, sa)
                    posq = kb.sb("posq", [4, S], BF16, sa)
                    NEB = 6; NEZ = 5; NSP = 4; NE2 = 3
                    Eb = [kb.sb(f"E{i}", [128, 512], BF16, sa) for i in range(NEB)]
                    ez = [kb.sb(f"ez{i}", [128, 512], F32, sa) for i in range(NEZ)]
                    spb = [kb.sb(f"sp{i}", [128, 512], BF16, sa) for i in range(NSP)]
                    Rb = kb.sb("Rb", [128, 512], BF16, sa)
                    trib = kb.sb("trib", [128, 128], BF16, sa)
                    ZbA = Buf("ZbA"); ZbB = Buf("ZbB")
                    e2 = [kb.sb(f"e2{i}", [128, 512], F32, sa) for i in range(NE2)]
                    R = kb.sb("R", [128, 512], F32, sa)
                    lamt = kb.sb("lamt", [128, 4, 64], F32, sa)
                    lamp = kb.sb("lamp", [128, 2, 64], F32, sa)
                    lams = kb.sb("lams", [128, 4], F32, sa)
                    gsc = kb.sb("gsc", [128, 128], F32, sa)
                    rz = kb.sb("rz", [128, 8], F32, sa)
                    ss = kb.sb("ss", [128, 4], F32, sa)
                    rstd = kb.sb("rstd", [128, 4], F32, sa)
                    tq = [kb.sb(f"tq{i}", [128, 128], F32, sa) for i in range(2)]
                    dq_ = [kb.sb(f"dq{i}", [128, 128], F32, sa) for i in range(4)]
                    dn = [kb.sb(f"dn{i}", [128, 128], F32, sa) for i in range(2)]
                    junk = kb.sb("junk", [128, 128], F32, sa)
                    osb = kb.sb("osb", [128, 512], F32, sa)
                    g1 = kb.sb("g1", [128, D], F32, sa)
                    b1 = kb.sb("b1", [128, D], F32, sa)
                    lntmp = kb.sb("lntmp", [128, D], F32, sa)
                    st = kb.sb("st", [128, 2, 6], F32, sa)
                    mv = kb.sb("mv", [128, 2], F32, sa)
                    r2 = kb.sb("r2", [128, 2], F32, sa)
                    rmseps = kb.sb("rmseps", [128, 1], F32, sa)

                    kb.dma("sp", "c1", maskc.t[:], c_maskc.ap(), writes=[maskc])
                    kb.dma("sp", "c1", masks.t[:], c_masks.ap(), writes=[masks])
                    kb.dma("sp", "c1", posk.t[:], c_posk.ap(), writes=[posk])
                    kb.dma("sp", "c1", lamt.t[:].rearrange("p a b -> p (a b)"),
                           bc(lam_h[l:l + 1, :, :].rearrange("o a b -> o (a b)"), [128, 256]), writes=[lamt])
                    kb.dma("sp", "c1", gsc.t[:], bc(subg_h[l:l + 1, :], [128, 128]), writes=[gsc])
                    kb.dma("sp", "c1", g1.t[:], bc(lng_h[l, 0:1, :], [128, D]), writes=[g1])
                    kb.dma("sp", "c1", b1.t[:], bc(lng_h[l, 1:2, :], [128, D]), writes=[b1])
                    op("pool", lambda e: e.memset(rmseps.t[:], RMS_EPS), writes=[rmseps])
                    op("pool", lambda e: e.tensor_copy(out=trib.t[:], in_=tri.t[:]), reads=[tri], writes=[trib])
                    op("dve", lambda e: e.tensor_tensor(out=lamp.t[:, 0, :], in0=lamt.t[:, 0, :], in1=lamt.t[:, 1, :], op=ALU.mult), reads=[lamt], writes=[lamp])
                    op("dve", lambda e: e.tensor_tensor(out=lamp.t[:, 1, :], in0=lamt.t[:, 2, :], in1=lamt.t[:, 3, :], op=ALU.mult), reads=[lamt], writes=[lamp])
                    op("dve", lambda e: e.tensor_reduce(out=lams.t[:, 0:2], in_=lamp.t[:], axis=AX.X, op=ALU.add), reads=[lamp], writes=[lams])
                    op("act", lambda e: e.activation(out=lams.t[:, 0:2], in_=lams.t[:, 0:2], func=AF.Exp), reads=[lams], writes=[lams])
                    op("dve", lambda e: e.tensor_tensor(out=lams.t[:, 2:3], in0=lams.t[:, 1:2], in1=lams.t[:, 0:1], op=ALU.subtract), reads=[lams], writes=[lams])
                    op("dve", lambda e: e.tensor_scalar(out=lams.t[:, 3:4], in0=lams.t[:, 2:3], scalar1=-li, scalar2=None, op0=ALU.add), reads=[lams], writes=[lams])
                    mlam = lams.t[:, 3:4]
                    op("act", lambda e: e.activation(out=gsc.t[:], in_=gsc.t[:], func=AF.Copy, scale=(1.0 - li)), reads=[gsc], writes=[gsc])

                    for g in range(8):
                        is_diff = g < 4
                        if is_diff:
                            cq, ck, cv = g * 128, 512 + g * 128, 1024 + g * 128
                        else:
                            cq, ck, cv = 1536 + (g - 4) * 128, 2048 + (g - 4) * 128, 2560 + (g - 4) * 128
                        wv = w_in_h[l].rearrange("(c p) n -> p c n", p=128)
                        for j, c0 in enumerate((cq, ck, cv)):
                            kb.dma("sp", "wst", wst.t[:, :, j * 128:(j + 1) * 128], wv[:, :, c0:c0 + 128], writes=[wst])
                        kb.dma("sp", "wost", wost.t[:], w_o_h[l, g * 128:(g + 1) * 128, :], writes=[wost])
                        op("dve", lambda e: e.tensor_copy(out=wbf.t[:], in_=wst.t[:]), reads=[wst], writes=[wbf])
                        op("dve", lambda e: e.tensor_copy(out=wobf.t[:], in_=wost.t[:]), reads=[wost], writes=[wobf])
                        if g == 0:
                            convert_layer(l)
                        if is_diff:
                            kb.dma("sp", "posq", posq.t[:, :], c_posq[:, g, :], writes=[posq])
                        for j, dst in ((0, QT), (1, KT)):
                            for blk in range(NQB):
                                bank = pb[(2 * j + blk) % 4]
                                for c in range(8):
                                    op("pe", lambda e: e.matmul(out=bank.t[:, 0:QBW], lhsT=wbf.t[:, c, j * 128:(j + 1) * 128],
                                                                rhs=hT.t[:, c, blk * QBW:(blk + 1) * QBW], start=(c == 0), stop=(c == 7)),
                                       reads=[wbf, hT], writes=[bank])
                                op("act", lambda e: e.activation(out=dst.t[:, blk * QBW:(blk + 1) * QBW], in_=bank.t[:, 0:QBW], func=AF.Copy),
                                   reads=[bank], writes=[dst])
                        for t4 in range(0, NT, 4):
                            bank = pb[(t4 // 4) % 4]
                            nt4 = min(4, NT - t4)
                            for tt in range(nt4):
                                t = t4 + tt
                                for c in range(8):
                                    op("pe", lambda e: e.matmul(out=bank.t[:, tt * 128:(tt + 1) * 128], lhsT=hT.t[:, c, t * 128:(t + 1) * 128],
                                                                rhs=wbf.t[:, c, 256:384], start=(c == 0), stop=(c == 7)),
                                       reads=[wbf, hT], writes=[bank])
                            op("act", lambda e: e.activation(out=V.t[:, t4:t4 + nt4, :], in_=bank.t[:, 0:nt4 * 128].rearrange("p (a b) -> p a b", b=128),
                                                             func=AF.Copy), reads=[bank], writes=[V])
                        steps = []
                        if is_diff:
                            for b in range(NQB):
                                q0 = b * QBW
                                kts = list(range(b * TPB + TPB))
                                if b % 2 == 0:
                                    O = [pb[4], pb[5]]; Zb = ZbA; zc = 0
                                else:
                                    O = [pb[2], pb[3]]; Zb = ZbB; zc = 8
                                for m in range(2):
                                    for ki, kt in enumerate(kts):
                                        first = (m == 0 and ki == 0); last = (m == 1 and ki == len(kts) - 1)

                                        def s1(b=b, q0=q0, m=m, kt=kt, sidx=len(steps)):
                                            pr = slice(m * 64, (m + 1) * 64)
                                            j = kt - b * TPB
                                            sbank = pb[sidx % 2]; E = Eb[sidx % NEB]
                                            op("pe", lambda e: e.matmul(out=sbank.t[:, 0:QBW], lhsT=KT.t[pr, kt * 128:(kt + 1) * 128],
                                                                        rhs=QT.t[pr, q0:q0 + QBW], start=True, stop=False),
                                               reads=[KT, QT], writes=[sbank])
                                            op("pe", lambda e: e.matmul(out=sbank.t[:, 0:QBW], lhsT=posk.t[0:4, kt * 128:(kt + 1) * 128],
                                                                        rhs=posq.t[0:4, q0:q0 + QBW], start=False, stop=True),
                                               reads=[posk, posq], writes=[sbank])
                                            op("act", lambda e: e.activation(out=E.t[:, 0:QBW], in_=sbank.t[:, 0:QBW], func=AF.Exp, scale=SCALE),
                                               reads=[sbank], writes=[E])
                                            if j >= 0:
                                                op("dve", lambda e: e.tensor_tensor(out=E.t[:, 0:QBW], in0=E.t[:, 0:QBW], in1=maskc.t[:, j, 0:QBW], op=ALU.mult),
                                                   reads=[E, maskc], writes=[E])

                                        def s2(b=b, q0=q0, m=m, kt=kt, sidx=len(steps), first=first, last=last, O=O, Zb=Zb, zc=zc, kts=kts):
                                            j = kt - b * TPB
                                            E = Eb[sidx % NEB]
                                            if first:
                                                for mm in range(2):
                                                    op("pe", lambda e: e.matmul(out=O[mm].t[:, :], lhsT=zeros.t[:, 0:128], rhs=zeros.t[:, :], start=True, stop=False),
                                                       reads=[zeros], writes=[O[mm]])
                                                op("pe", lambda e: e.matmul(out=pb[6].t[:, zc:zc + 8], lhsT=zeros.t[:, 0:128], rhs=zeros.t[:, 0:8], start=True, stop=False),
                                                   reads=[zeros], writes=[Zb])
                                            for tt in range(max(j, 0), TPB):
                                                op("pe", lambda e: e.matmul(out=O[m].t[:, tt * 128:(tt + 1) * 128], lhsT=E.t[:, tt * 128:(tt + 1) * 128],
                                                                            rhs=V.t[:, kt, :], start=False, stop=(kt == kts[-1]), skip_group_check=True),
                                                   reads=[E, V], writes=[O[m]])
                                                op("pe", lambda e: e.matmul(out=pb[6].t[:, zc + m * 4 + tt:zc + m * 4 + tt + 1], lhsT=E.t[:, tt * 128:(tt + 1) * 128],
                                                                            rhs=onesb.t[:, 0:1], start=False, stop=(kt == kts[-1]), skip_group_check=True),
                                                   reads=[E, onesb], writes=[Zb])
                                            if not last:
                                                return
                                            op("dve", lambda e: e.reciprocal(out=rz.t[:], in_=pb[6].t[:, zc:zc + 8]), reads=[Zb], writes=[rz])
                                            op("dve", lambda e: e.tensor_scalar(out=rz.t[:, 4:8], in0=rz.t[:, 4:8], scalar1=mlam, scalar2=None, op0=ALU.mult),
                                               reads=[rz], sreads=[lams], writes=[rz])
                                            for tt in range(TPB):
                                                tqb = tq[tt % 2]; dd = dq_[tt]
                                                op("act", lambda e: e.activation(out=tqb.t[:], in_=O[0].t[:, tt * 128:(tt + 1) * 128], func=AF.Copy, scale=rz.t[:, tt:tt + 1]),
                                                   reads=[O[0]], sreads=[rz], writes=[tqb])
                                                op("dve", lambda e: e.scalar_tensor_tensor(out=dd.t[:], in0=O[1].t[:, tt * 128:(tt + 1) * 128], scalar=rz.t[:, 4 + tt:5 + tt],
                                                                                           in1=tqb.t[:], op0=ALU.mult, op1=ALU.add), reads=[O[1], tqb], sreads=[rz], writes=[dd])
                                                op("act", lambda e: e.activation(out=junk.t[:], in_=dd.t[:], func=AF.Square, accum_out=ss.t[:, tt:tt + 1]),
                                                   reads=[dd], writes=[junk, ss])
                                            op("act", lambda e: e.activation(out=rstd.t[:, 0:TPB], in_=ss.t[:, 0:TPB], func=AF.Ln, scale=1.0 / 128, bias=rmseps.t[:, 0:1]),
                                               reads=[ss, rmseps], writes=[rstd])
                                            op("act", lambda e: e.activation(out=rstd.t[:, 0:TPB], in_=rstd.t[:, 0:TPB], func=AF.Exp, scale=-0.5), reads=[rstd], writes=[rstd])
                                            tb = pb[7]
                                            for tt in range(TPB):
                                                dnb = dn[tt % 2]
                                                op("dve", lambda e: e.scalar_tensor_tensor(out=dnb.t[:], in0=dq_[tt].t[:], scalar=rstd.t[:, tt:tt + 1], in1=gsc.t[:],
                                                                                           op0=ALU.mult, op1=ALU.mult), reads=[dq_[tt], gsc], sreads=[rstd], writes=[dnb])
                                                op("pe", lambda e: e.transpose(out=tb.t[:, tt * 128:(tt + 1) * 128], in_=dnb.t[:], identity=ident.t[:]),
                                                   reads=[dnb, ident], writes=[tb])
                                            op("act", lambda e: e.activation(out=mixT.t[:, q0:q0 + QBW], in_=tb.t[:, 0:QBW], func=AF.Copy), reads=[tb], writes=[mixT])
                                        steps.append((s1, s2))
                            SKEW = (0, 3)
                        else:
                            for b in range(NQB):
                                q0 = b * QBW
                                kts = list(range(b * TPB + TPB))
                                O = pb[4 + b % 2]
                                for p in range(2):
                                    for si, kt in enumerate(reversed(kts)):
                                        first = (p == 0 and si == 0); last = (p == 1 and kt == 0)

                                        def s1(b=b, q0=q0, p=p, kt=kt, sidx=len(steps)):
                                            pr = slice(p * 64, (p + 1) * 64)
                                            j = kt - b * TPB
                                            zbank = pb[sidx % 2]; ezb = ez[sidx % NEZ]; sp_ = spb[sidx % NSP]
                                            op("pe", lambda e: e.matmul(out=zbank.t[:, 0:QBW], lhsT=KT.t[pr, kt * 128:(kt + 1) * 128],
                                                                        rhs=QT.t[pr, q0:q0 + QBW], start=True, stop=True), reads=[KT, QT], writes=[zbank])
                                            op("act", lambda e: e.activation(out=ezb.t[:, 0:QBW], in_=zbank.t[:, 0:QBW], func=AF.Exp, scale=SCALE),
                                               reads=[zbank], writes=[ezb])
                                            op("act", lambda e: e.activation(out=sp_.t[:, 0:QBW], in_=ezb.t[:, 0:QBW], func=AF.Ln, bias=ones.t[:, 0:1], scale=1.0),
                                               reads=[ezb, ones], writes=[sp_])
                                            if j >= 0:
                                                op("dve", lambda e: e.tensor_tensor(out=sp_.t[:, 0:QBW], in0=sp_.t[:, 0:QBW], in1=masks.t[:, j, 0:QBW], op=ALU.mult),
                                                   reads=[sp_, masks], writes=[sp_])

                                        def s2(b=b, kt=kt, si=si, sidx=len(steps)):
                                            cbank = pb[2 + sidx % 2]; sp_ = spb[sidx % NSP]; e2b = e2[sidx % NE2]
                                            op("pe", lambda e: e.matmul(out=cbank.t[:, 0:QBW], lhsT=trib.t[:], rhs=sp_.t[:, 0:QBW], start=True, stop=(si == 0)),
                                               reads=[trib, sp_], writes=[cbank])
                                            if si > 0:
                                                op("pe", lambda e: e.matmul(out=cbank.t[:, 0:QBW], lhsT=onesb.t[:], rhs=Rb.t[:, 0:QBW], start=False, stop=True),
                                                   reads=[onesb, Rb], writes=[cbank])
                                            if si == 0:
                                                op("dve", lambda e: e.tensor_copy(out=R.t[:, 0:QBW], in_=sp_.t[:, 0:QBW]), reads=[sp_], writes=[R])
                                                if kt > 0:
                                                    op("dve", lambda e: e.tensor_copy(out=Rb.t[:, 0:QBW], in_=sp_.t[:, 0:QBW]), reads=[sp_], writes=[Rb])
                                            elif kt > 0:
                                                op("dve", lambda e: e.tensor_tensor(out=R.t[:, 0:QBW], in0=R.t[:, 0:QBW], in1=sp_.t[:, 0:QBW], op=ALU.add),
                                                   reads=[R, sp_], writes=[R])
                                                op("dve", lambda e: e.tensor_copy(out=Rb.t[:, 0:QBW], in_=R.t[:, 0:QBW]), reads=[R], writes=[Rb])
                                            op("act", lambda e: e.activation(out=e2b.t[:, 0:QBW], in_=cbank.t[:, 0:QBW], func=AF.Exp, scale=-1.0),
                                               reads=[cbank], writes=[e2b])

                                        def s3(b=b, q0=q0, p=p, kt=kt, sidx=len(steps), first=first, last=last, O=O):
                                            j = kt - b * TPB
                                            ezb = ez[sidx % NEZ]; e2b = e2[sidx % NE2]; W = Eb[sidx % NEB]
                                            op("dve", lambda e: e.tensor_tensor(out=W.t[:, 0:QBW], in0=ezb.t[:, 0:QBW], in1=e2b.t[:, 0:QBW], op=ALU.mult),
                                               reads=[ezb, e2b], writes=[W])
                                            if j >= 0:
                                                op("dve", lambda e: e.tensor_tensor(out=W.t[:, 0:QBW], in0=W.t[:, 0:QBW], in1=masks.t[:, j, 0:QBW], op=ALU.mult),
                                                   reads=[W, masks], writes=[W])
                                            if first:
                                                op("pe", lambda e: e.matmul(out=O.t[:, :], lhsT=zeros.t[:, 0:128], rhs=zeros.t[:, :], start=True, stop=False),
                                                   reads=[zeros], writes=[O])
                                            for tt in range(max(j, 0), TPB):
                                                op("pe", lambda e: e.matmul(out=O.t[:, tt * 128 + p * 64:tt * 128 + (p + 1) * 64], lhsT=W.t[:, tt * 128:(tt + 1) * 128],
                                                                            rhs=V.t[:, kt, p * 64:(p + 1) * 64], start=False, stop=(kt == 0), skip_group_check=True),
                                                   reads=[W, V], writes=[O])
                                            if not last:
                                                return
                                            op("act", lambda e: e.activation(out=osb.t[:, 0:QBW], in_=O.t[:, 0:QBW], func=AF.Copy), reads=[O], writes=[osb])
                                            tb = pb[7]
                                            for tt in range(TPB):
                                                op("pe", lambda e: e.transpose(out=tb.t[:, tt * 128:(tt + 1) * 128], in_=osb.t[:, tt * 128:(tt + 1) * 128], identity=ident.t[:]),
                                                   reads=[osb, ident], writes=[tb])
                                            op("act", lambda e: e.activation(out=mixT.t[:, q0:q0 + QBW], in_=tb.t[:, 0:QBW], func=AF.Copy), reads=[tb], writes=[mixT])
                                        steps.append((s1, s2, s3))
                            SKEW = (0, 2, 4)
                        for n in range(len(steps) + max(SKEW)):
                            for k, sk in enumerate(SKEW):
                                i = n - sk
                                if 0 <= i < len(steps):
                                    steps[i][k]()
                        for t in range(NT):
                            for half in range(2):
                                bank = pb[2 + (2 * t + half) % 2]
                                op("pe", lambda e: e.matmul(out=bank.t[:, :], lhsT=mixT.t[:, t * 128:(t + 1) * 128], rhs=wobf.t[:, half * 512:(half + 1) * 512],
                                                            start=True, stop=True), reads=[mixT, wobf], writes=[bank])
                                hs = h.t[:, t, half * 512:(half + 1) * 512]
                                if g == 0:
                                    op("dve", lambda e: e.scalar_tensor_tensor(out=hs, in0=hs, scalar=ALPHA, in1=bank.t[:, :], op0=ALU.mult, op1=ALU.add),
                                       reads=[bank, ht[t]], writes=[ht[t]])
                                else:
                                    op("dve", lambda e: e.tensor_tensor(out=hs, in0=hs, in1=bank.t[:, :], op=ALU.add), reads=[bank, ht[t]], writes=[ht[t]])
                    for t in range(NT):
                        layer_norm_tile(t, h.t[:, t, :], g1, b1, lntmp, st, mv, r2)
                    kb.barrier()
            if dbg and l == 0:
                hv = dbg_h["h1"].ap().rearrange("(n p) d -> p n d", p=128)
                for t in range(NT):
                    kb.dma("sp", "dbg", hv[:, t, :], h.t[:, t, :], reads=[ht[t]])
            if do_peer:
                if not do_attn:
                    convert_layer(l)
                with ExitStack() as sp1:
                    top_s = kb.sb("top_s", [128, NT, 16, 16], F32, sp1)
                    top_i = kb.sb("top_i", [128, NT, 16, 16], U16, sp1)
                    with ExitStack() as sq:
                        hT = kb.sb("hT", [128, 8, S], BF16, sq)
                        build_hT(hT)
                        kst = kb.sb("kst", [128, 16, 128], F32, sq)
                        keysT = kb.sb("keysT", [128, 16, 128], BF16, sq)
                        wqst = [kb.sb(f"wqst{i}", [128, 8, 128], F32, sq) for i in range(2)]
                        wqbf = [kb.sb(f"wqbf{i}", [128, 8, 128], BF16, sq) for i in range(2)]
                        qT = [kb.sb(f"qT{i}", [128, S], BF16, sq) for i in range(2)]
                        sc = [kb.sb(f"sc{i}", [128, S], F32, sq) for i in range(2)]
                        mrs = [kb.sb(f"mr{i}", [128, 128], F32, sq) for i in range(4)]
                        tsb = [Buf(f"tsb{i}") for i in range(NT)]; tib = [Buf(f"tib{i}") for i in range(NT)]
                        kb.dma("sp", "kst", kst.t[:], keys_h[l].rearrange("a n c -> n a c"), writes=[kst])
                        for hp in range(16):
                            bank = pb[hp % 4]
                            op("pe", lambda e: e.transpose(out=bank.t[:, 0:128], in_=kst.t[:, hp, :], identity=ident.t[:]), reads=[kst, ident], writes=[bank])
                            op("act", lambda e: e.activation(out=keysT.t[:, hp, :], in_=bank.t[:, 0:128], func=AF.Copy), reads=[bank], writes=[keysT])
                        wqv = wq_h[l].rearrange("(c p) n -> p c n", p=128)
                        for hp in range(16):
                            ws = wqst[hp % 2]; wb = wqbf[hp % 2]; qt_ = qT[hp % 2]; scb = sc[hp % 2]
                            kb.dma("sp", f"wq{hp % 2}", ws.t[:], wqv[:, :, hp * 128:(hp + 1) * 128], writes=[ws])
                            op("pool", lambda e: e.tensor_copy(out=wb.t[:], in_=ws.t[:]), reads=[ws], writes=[wb])
                            for blk in range(NQB):
                                bank = pb[blk % 4]
                                for c in range(8):
                                    op("pe", lambda e: e.matmul(out=bank.t[:, 0:QBW], lhsT=wb.t[:, c, :], rhs=hT.t[:, c, blk * QBW:(blk + 1) * QBW],
                                                                start=(c == 0), stop=(c == 7)), reads=[wb, hT], writes=[bank])
                                op("act", lambda e: e.activation(out=qt_.t[:, blk * QBW:(blk + 1) * QBW], in_=bank.t[:, 0:QBW], func=AF.Copy),
                                   reads=[bank], writes=[qt_])
                            for blk in range(NQB):
                                bank = pb[4 + blk % 4]
                                for tt in range(TPB):
                                    t = blk * TPB + tt
                                    op("pe", lambda e: e.matmul(out=bank.t[:, tt * 128:(tt + 1) * 128], lhsT=qt_.t[:, t * 128:(t + 1) * 128], rhs=keysT.t[:, hp, :],
                                                                start=True, stop=True), reads=[qt_, keysT], writes=[bank])
                                op("act", lambda e: e.activation(out=scb.t[:, blk * QBW:(blk + 1) * QBW], in_=bank.t[:, 0:QBW], func=AF.Copy),
                                   reads=[bank], writes=[scb])
                            TG = 4
                            for t0_ in range(0, NT, TG):
                                tl = list(range(t0_, min(NT, t0_ + TG)))
                                sv = {t: scb.t[:, t * 128:(t + 1) * 128] for t in tl}
                                for t in tl:
                                    op("dve", lambda e: e.max(out=top_s.t[:, t, hp, 0:8], in_=sv[t]), reads=[scb], writes=[tsb[t]])
                                for t in tl:
                                    op("dve", lambda e: e.tensor_scalar(out=mrs[t % TG].t[:], in0=sv[t], scalar1=top_s.t[:, t, hp, 7:8], scalar2=None, op0=ALU.is_ge),
                                       reads=[scb], sreads=[tsb[t]], writes=[mrs[t % TG]])
                                for t in tl:
                                    op("dve", lambda e: e.scalar_tensor_tensor(out=mrs[t % TG].t[:], in0=mrs[t % TG].t[:], scalar=-1e30, in1=sv[t], op0=ALU.mult, op1=ALU.add),
                                       reads=[scb, mrs[t % TG]], writes=[mrs[t % TG]])
                                for t in tl:
                                    op("dve", lambda e: e.max(out=top_s.t[:, t, hp, 8:16], in_=mrs[t % TG].t[:]), reads=[mrs[t % TG]], writes=[tsb[t]])
                                for t in tl:
                                    op("dve", lambda e: e.max_index(out=top_i.t[:, t, hp, 0:8], in_max=top_s.t[:, t, hp, 0:8], in_values=sv[t]),
                                       reads=[scb, tsb[t]], writes=[tib[t]])
                                for t in tl:
                                    op("dve", lambda e: e.max_index(out=top_i.t[:, t, hp, 8:16], in_max=top_s.t[:, t, hp, 8:16], in_values=sv[t]),
                                       reads=[scb, tsb[t]], writes=[tib[t]])
                        for t in range(NT):
                            top_s.w = tsb[t].w if top_s.w is None or (tsb[t].w and tsb[t].w[1] > top_s.w[1]) else top_s.w
                            top_i.w = tib[t].w if top_i.w is None or (tib[t].w and tib[t].w[1] > top_i.w[1]) else top_i.w
                        kb.barrier()
                    with ExitStack() as sg:
                        tif = kb.sb("tif", [128, 16, 16], F32, sg)
                        cand = kb.sb("cand", [128, 8, 16, 16], F32, sg)
                        mr2 = kb.sb("mr2", [128, 256], F32, sg)
                        bs = kb.sb("bs", [128, 8, 16], F32, sg)
                        bp = kb.sb("bp", [128, 8, 16], U32, sg)
                        ab_i = kb.sb("ab_i", [128, 2, 128], U32, sg)
                        ab_f = kb.sb("ab_f", [128, 2, 8, 16], F32, sg)
                        oh = kb.sb("oh", [128, 8, 16, 16], F32, sg)
                        sel = kb.sb("sel", [128, 2, 128], F32, sg)
                        idxf = kb.sb("idxf", [128, 128], F32, sg)
                        idx = [kb.sb(f"idx{i}", [128, 128], I32, sg) for i in range(2)]
                        gate = [kb.sb(f"gate{i}", [128, 8, 16], F32, sg) for i in range(2)]
                        gsum = kb.sb("gsum", [128, 8], F32, sg)
                        act = [kb.sb(f"act{i}", [128, 128], F32, sg) for i in range(2)]
                        ga = [kb.sb(f"ga{i}", [128, 128], F32, sg) for i in range(2)]
                        gb_ = [kb.sb(f"gb{i}", [128, 2048], BF16, sg) for i in range(NB_G)]
                        prod = [kb.sb(f"prod{i}", [128, D], BF16, sg) for i in range(2)]
                        xbf = [kb.sb(f"xbf{i}", [128, D], BF16, sg) for i in range(2)]
                        y = kb.sb("y", [128, D], F32, sg)
                        junkb = kb.sb("junkb", [128, D], BF16, sg)
                        g2 = kb.sb("g2", [128, D], F32, sg)
                        b2 = kb.sb("b2", [128, D], F32, sg)
                        st = kb.sb("st", [128, 2, 6], F32, sg)
                        mv = kb.sb("mv", [128, 2], F32, sg)
                        r2 = kb.sb("r2", [128, 2], F32, sg)
                        kb.dma("sp", "c2", g2.t[:], bc(lng_h[l, 2:3, :], [128, D]), writes=[g2])
                        kb.dma("sp", "c2", b2.t[:], bc(lng_h[l, 3:4, :], [128, D]), writes=[b2])

                        def p2_ops(t):
                            ix = idx[t % 2]; gt_ = gate[t % 2]
                            ts4 = top_s.t[:, t, :, :].rearrange("p (h two) k -> p h two k", two=2)
                            tif4 = tif.t[:].rearrange("p (h two) k -> p h two k", two=2)
                            T = []
                            T.append(lambda: op("dve", lambda e: e.tensor_copy(out=tif.t[:], in_=top_i.t[:, t, :, :]), reads=[top_i], writes=[tif]))
                            T.append(lambda: op("dve", lambda e: e.tensor_tensor(out=cand.t[:], in0=bc(ts4[:, :, 0, :].unsqueeze(3), [128, 8, 16, 16]),
                                                                                 in1=bc(ts4[:, :, 1, :].unsqueeze(2), [128, 8, 16, 16]), op=ALU.add),
                                                reads=[top_s], writes=[cand]))
                            for hh in range(8):
                                def f(hh=hh):
                                    cv_ = cand.t[:, hh, :, :].rearrange("p a b -> p (a b)")
                                    op("dve", lambda e: e.max(out=bs.t[:, hh, 0:8], in_=cv_), reads=[cand], writes=[bs])
                                    op("dve", lambda e: e.tensor_scalar(out=mr2.t[:], in0=cv_, scalar1=bs.t[:, hh, 7:8], scalar2=None, op0=ALU.is_ge),
                                       reads=[cand], sreads=[bs], writes=[mr2])
                                    op("dve", lambda e: e.scalar_tensor_tensor(out=mr2.t[:], in0=mr2.t[:], scalar=-1e30, in1=cv_, op0=ALU.mult, op1=ALU.add),
                                       reads=[cand, mr2], writes=[mr2])
                                    op("dve", lambda e: e.max(out=bs.t[:, hh, 8:16], in_=mr2.t[:]), reads=[mr2], writes=[bs])
                                    op("dve", lambda e: e.max_index(out=bp.t[:, hh, 0:8], in_max=bs.t[:, hh, 0:8], in_values=cv_), reads=[cand, bs], writes=[bp])
                                    op("dve", lambda e: e.max_index(out=bp.t[:, hh, 8:16], in_max=bs.t[:, hh, 8:16], in_values=cv_), reads=[cand, bs], writes=[bp])
                                T.append(f)
                            bpf = bp.t[:].rearrange("p h k -> p (h k)")
                            T.append(lambda: op("dve", lambda e: e.tensor_single_scalar(out=ab_i.t[:, 0, :], in_=bpf, scalar=4, op=ALU.logical_shift_right), reads=[bp], writes=[ab_i]))
                            T.append(lambda: op("dve", lambda e: e.tensor_single_scalar(out=ab_i.t[:, 1, :], in_=bpf, scalar=15, op=ALU.bitwise_and), reads=[bp], writes=[ab_i]))
                            T.append(lambda: op("dve", lambda e: e.tensor_copy(out=ab_f.t[:].rearrange("p a h k -> p a (h k)"), in_=ab_i.t[:]), reads=[ab_i], writes=[ab_f]))
                            for w in range(2):
                                T.append(lambda w=w: op("dve", lambda e: e.tensor_tensor(out=oh.t[:], in0=bc(ab_f.t[:, w, :, :].unsqueeze(3), [128, 8, 16, 16]),
                                                                                         in1=bc(iota16.t[:].unsqueeze(1).unsqueeze(1), [128, 8, 16, 16]), op=ALU.is_equal),
                                                        reads=[ab_f, iota16], writes=[oh]))
                                T.append(lambda w=w: op("dve", lambda e: e.tensor_tensor(out=oh.t[:], in0=oh.t[:], in1=bc(tif4[:, :, w, :].unsqueeze(2), [128, 8, 16, 16]), op=ALU.mult),
                                                        reads=[oh, tif], writes=[oh]))
                                T.append(lambda w=w: op("dve", lambda e: e.tensor_reduce(out=sel.t[:, w, :], in_=oh.t[:].rearrange("p h k a -> p (h k) a"), axis=AX.X, op=ALU.add),
                                                        reads=[oh], writes=[sel]))
                            T.append(lambda: op("dve", lambda e: e.tensor_scalar(out=idxf.t[:], in0=sel.t[:, 0, :], scalar1=128.0, scalar2=float(l * NEXP), op0=ALU.mult, op1=ALU.add),
                                                reads=[sel], writes=[idxf]))
                            T.append(lambda: op("dve", lambda e: e.tensor_tensor(out=idxf.t[:], in0=idxf.t[:], in1=sel.t[:, 1, :], op=ALU.add), reads=[idxf, sel], writes=[idxf]))
                            T.append(lambda: op("dve", lambda e: e.tensor_copy(out=ix.t[:], in_=idxf.t[:]), reads=[idxf], writes=[ix]))
                            T.append(lambda: op("dve", lambda e: e.tensor_tensor(out=gt_.t[:], in0=bs.t[:], in1=bc(bs.t[:, :, 0:1], [128, 8, 16]), op=ALU.subtract), reads=[bs], writes=[gt_]))
                            T.append(lambda: op("act", lambda e: e.activation(out=gt_.t[:], in_=gt_.t[:], func=AF.Exp), reads=[gt_], writes=[gt_]))
                            T.append(lambda: op("dve", lambda e: e.tensor_reduce(out=gsum.t[:], in_=gt_.t[:], axis=AX.X, op=ALU.add), reads=[gt_], writes=[gsum]))
                            T.append(lambda: op("dve", lambda e: e.reciprocal(out=gsum.t[:], in_=gsum.t[:]), reads=[gsum], writes=[gsum]))
                            T.append(lambda: op("dve", lambda e: e.tensor_tensor(out=gt_.t[:], in0=gt_.t[:], in1=bc(gsum.t[:].unsqueeze(2), [128, 8, 16]), op=ALU.mult),
                                                reads=[gt_, gsum], writes=[gt_]))
                            if t == 0:
                                T.append(lambda: (dump("top_s", top_s.t[:, 0, :, :], [128, 16, 16], F32, [top_s]),
                                                  dump("idx", ix.t[:], [128, 128], I32, [ix]),
                                                  dump("gate", gt_.t[:], [128, 8, 16], F32, [gt_])))
                            return T

                        for th in p2_ops(0):
                            th()
                        gcount = 0; pcount = 0; dcount = 0
                        Dg = [kb.sb(f"Dg{i}", [128, 128], BF16, sg) for i in range(6)]
                        EB = 8
                        NBATCH = 128 // EB
                        for t in range(NT):
                            ix = idx[t % 2]; gt_ = gate[t % 2]; at = act[t % 2]; gat = ga[t % 2]; xb = xbf[t % 2]
                            nxt = p2_ops(t + 1) if t + 1 < NT else []
                            per = -(-len(nxt) // NBATCH) if nxt else 0
                            op("act", lambda e: e.activation(out=xb.t[:], in_=h.t[:, t, :], func=AF.Copy), reads=[ht[t]], writes=[xb])
                            slots = {}
                            for b in range(NBATCH + 1):
                                if b >= 1:
                                    sl = slice((b - 1) * EB, b * EB)
                                    op("act", lambda e: e.activation(out=gat.t[:, sl], in_=at.t[:, sl], func=AF.Gelu), reads=[at], writes=[gat])
                                    op("dve", lambda e: e.tensor_tensor(out=gat.t[:, sl], in0=gat.t[:, sl], in1=gt_.t[:].rearrange("p h k -> p (h k)")[:, sl], op=ALU.mult),
                                       reads=[gat, gt_], writes=[gat])
                                if b >= 1:
                                    for ei in range((b - 1) * EB, b * EB):
                                        gbuf = gb_[slots[ei]]
                                        dg = Dg[dcount % len(Dg)]; dcount += 1
                                        op("dve", lambda e: e.tensor_scalar(out=dg.t[:], in0=identb.t[:], scalar1=gat.t[:, ei:ei + 1], scalar2=None, op0=ALU.mult),
                                           reads=[identb], sreads=[gat], writes=[dg])
                                        for half in range(2):
                                            yb = pb[2 * (t % 2) + half]
                                            op("pe", lambda e: e.matmul(out=yb.t[:, :], lhsT=dg.t[:], rhs=gbuf.t[:, D + half * 512:D + (half + 1) * 512],
                                                                        start=(ei == 0), stop=(ei == 127)), reads=[dg, gbuf], writes=[yb])
                                if b < NBATCH:
                                    for ei in range(b * EB, (b + 1) * EB):
                                        s_ = gcount % NB_G; gcount += 1; slots[ei] = s_
                                        gbuf = gb_[s_]
                                        kb.dma("pool", f"g{s_}", gbuf.t[:, :], uv_h[:, :], reads=[ix, scr[l]], writes=[gbuf], indirect=ix.t[:, ei:ei + 1])
                                        pr_ = prod[pcount % 2]; pcount += 1
                                        op("dve", lambda e: e.tensor_tensor(out=pr_.t[:], in0=gbuf.t[:, 0:D], in1=xb.t[:], op=ALU.mult), reads=[gbuf, xb], writes=[pr_])
                                        op("act", lambda e: e.activation(out=junkb.t[:], in_=pr_.t[:], func=AF.Copy, accum_out=at.t[:, ei:ei + 1]),
                                           reads=[pr_], writes=[junkb, at])
                                for _ in range(per):
                                    if nxt:
                                        nxt.pop(0)()
                            while nxt:
                                nxt.pop(0)()
                            if t == 0:
                                dump("act", at.t[:], [128, 128], F32, [at])
                            for half in range(2):
                                yb = pb[2 * (t % 2) + half]
                                op("dve", lambda e: e.scalar_tensor_tensor(out=y.t[:, half * 512:(half + 1) * 512], in0=h.t[:, t, half * 512:(half + 1) * 512], scalar=ALPHA,
                                                                           in1=yb.t[:, :], op0=ALU.mult, op1=ALU.add), reads=[ht[t], yb], writes=[y])
                            layer_norm_tile(t, y.t[:], g2, b2, y, st, mv, r2)
                        kb.barrier()
        ov = out_h.ap().rearrange("(n p) d -> p n d", p=128)
        for t in range(NT):
            kb.dma("sp", "out", ov[:, t, :], h.t[:, t, :], reads=[ht[t]])
        kb.barrier()
    return nc


def core_inputs(S, L, x_b, P, consts):
    m = {"x": np.ascontiguousarray(x_b.reshape(S, D))}
    m.update(P)
    m.update({"c_" + k: v for k, v in consts.items()})
    return m


def pack_params(inputs, L):
    f = lambda a: np.ascontiguousarray(np.asarray(a, dtype=np.float32))
    P = {
        "w_in": f(inputs["w_in"][:L]),
        "lam4": f(np.stack([inputs["lam_q1"][:L], inputs["lam_k1"][:L], inputs["lam_q2"][:L], inputs["lam_k2"][:L]], axis=1)),
        "subln_g": f(inputs["subln_g"][:L]),
        "w_o": f(inputs["w_o"][:L]),
        "ln_gb": f(np.stack([inputs["ln1_g"][:L], inputs["ln1_b"][:L], inputs["ln2_g"][:L], inputs["ln2_b"][:L]], axis=1)),
        "w_query": f(inputs["w_query"][:L]),
        "sub_keys": f(np.asarray(inputs["sub_keys"][:L]).reshape(L, 16, 128, 128)),
        "expert_u": f(np.asarray(inputs["expert_u"][:L]).reshape(L * NEXP, D)),
        "expert_v": f(np.asarray(inputs["expert_v"][:L]).reshape(L * NEXP, D)),
    }
    return P


def kernel(**inputs):
    x = np.asarray(inputs["x"], dtype=np.float32)
    B, S, _ = x.shape
    L = DEPTH
    P = pack_params(inputs, L)
    consts = host_consts(S)
    nc = build_program(S, L)
    in_maps = [core_inputs(S, L, x[b], P, consts) for b in range(B)]
    res = run_bass_kernel_spmd(nc, in_maps, core_ids=list(range(B)))
    out = np.stack([np.asarray(r["out"]).reshape(S, D) for r in res.results], axis=0)
    return out.astype(np.float32)
```

```python
import math
import numpy as np
from contextlib import ExitStack
import ml_dtypes
import concourse.bass as bass
import concourse.mybir as mybir
from concourse.bass_utils import run_bass_kernel_spmd

F32 = mybir.dt.float32; BF16 = mybir.dt.bfloat16; I32 = mybir.dt.int32; U32 = mybir.dt.uint32; U16 = mybir.dt.uint16
AF = mybir.ActivationFunctionType; ALU = mybir.AluOpType; AX = mybir.AxisListType

D = 1024; DEPTH = 4; NEXP = 16384
ALPHA = (2.0 * DEPTH) ** 0.25
LN_EPS = 1e-5; RMS_EPS = 1e-5
SCALE = 0.125
NB_G = 16


class Buf:
    def __init__(self, name, t=None):
        self.name = name; self.t = t; self.w = None; self.r = {}

    def __getitem__(self, k):
        return self.t[k]


class DSem:
    def __init__(self, sem):
        self.sem = sem; self.count = 0


class Eng:
    def __init__(self, name, eng, sem):
        self.name = name; self.eng = eng; self.sem = sem; self.count = 0; self.seen = {}


class KB:
    def __init__(self, nc, es):
        self.nc = nc; self.es = es
        self.E = {}
        for n, e in (("pe", nc.tensor), ("act", nc.scalar), ("dve", nc.vector), ("pool", nc.gpsimd), ("sp", nc.sync)):
            self.E[n] = Eng(n, e, es.enter_context(nc.semaphore("sem_" + n)))
        self.dsems = {}
        self.semobj = {}
        for E in self.E.values():
            self.semobj[id(E.sem)] = (E.sem, E)
        self.uid = 0

    def dsem(self, name):
        if name not in self.dsems:
            d = DSem(self.es.enter_context(self.nc.semaphore("ds_" + name)))
            self.dsems[name] = d; self.semobj[id(d.sem)] = (d.sem, d)
        return self.dsems[name]

    def sb(self, name, shape, dtype, es=None):
        self.uid += 1
        t = (es or self.es).enter_context(self.nc.sbuf_tensor(f"{name}_{self.uid}", list(shape), dtype))
        return Buf(name, t)

    def ps(self, name, shape, dtype, es=None):
        self.uid += 1
        t = (es or self.es).enter_context(self.nc.psum_tensor(f"{name}_{self.uid}", list(shape), dtype))
        return Buf(name, t)

    def _wait(self, E, toks, strict=()):
        best = {}
        for (sid, v) in list(toks) + list(strict):
            if best.get(sid, 0) < v:
                best[sid] = v
        sown = 0
        for (sid, v) in strict:
            if self.semobj[sid][1] is E:
                sown = max(sown, v)
        for sid, v in best.items():
            sem, owner = self.semobj[sid]
            if isinstance(owner, DSem):
                v = owner.count
            if owner is E and E.name == "pe":
                continue
            if E.seen.get(sid, 0) < v:
                E.eng.wait_ge(sem, v); E.seen[sid] = v

    def _deps(self, reads, writes):
        toks = []
        for b in reads:
            if b.w:
                toks.append(b.w)
        for b in writes:
            if b.w:
                toks.append(b.w)
            toks.extend(b.r.items())
        return toks

    def _mark(self, tok, reads, writes):
        sid, v = tok
        for b in reads:
            if b.r.get(sid, 0) < v:
                b.r[sid] = v
        for b in writes:
            b.w = tok; b.r = {}

    def op(self, en, fn, reads=(), writes=(), sreads=()):
        E = self.E[en]
        self._wait(E, self._deps(reads, writes), [b.w for b in sreads if b.w])
        ins = fn(E.eng)
        E.count += 1
        ins.then_inc(E.sem, 1)
        self._mark((id(E.sem), E.count), list(reads) + list(sreads), writes)
        return ins

    def dma(self, qn, dname, out, in_, reads=(), writes=(), indirect=None):
        E = self.E[qn]; d = self.dsem(dname)
        self._wait(E, self._deps(reads, writes))
        if indirect is not None:
            ins = E.eng.indirect_dma_start(out=out, out_offset=None, in_=in_,
                                           in_offset=bass.IndirectOffsetOnAxis(ap=indirect, axis=0))
        else:
            ins = E.eng.dma_start(out=out, in_=in_)
        d.count += 16
        ins.then_inc(d.sem, 16)
        self._mark((id(d.sem), d.count), reads, writes)
        return ins

    def barrier(self):
        toks = [(id(E.sem), E.count) for E in self.E.values() if E.count] + \
               [(id(d.sem), d.count) for d in self.dsems.values() if d.count]
        for E in self.E.values():
            self._wait(E, toks)


def bc(ap, shape):
    return ap.to_broadcast(list(shape))


def host_consts(S):
    c = {}
    c["ident"] = np.eye(128, dtype=np.float32)
    j = np.arange(128)
    c["tri"] = (j[:, None] >= j[None, :]).astype(np.float32)
    c["ones"] = np.ones((128, 128), np.float32)
    q = np.arange(512)
    mc = np.zeros((128, 4, 512), np.float32); ms = np.zeros((128, 4, 512), np.float32)
    for jj in range(4):
        mc[:, jj, :] = (q[None, :] >= jj * 128 + j[:, None])
        ms[:, jj, :] = (q[None, :] > jj * 128 + j[:, None])
    c["maskc"] = mc.astype(ml_dtypes.bfloat16); c["masks"] = (-800.0 * (1.0 - ms)).astype(ml_dtypes.bfloat16)
    pos = np.arange(S)
    hi = (pos // 128) * 128.0; lo = (pos % 128) * 1.0
    posk = np.zeros((4, S), np.float32); posk[0] = hi; posk[1] = lo; posk[2] = 1; posk[3] = 1
    c["posk"] = posk.astype(ml_dtypes.bfloat16)
    posq = np.zeros((4, 4, S), np.float32)
    for hh in range(4):
        cc = 2.0 ** (-8.0 * (hh + 1) / 4) / SCALE
        posq[0, hh] = cc; posq[1, hh] = cc; posq[2, hh] = -cc * hi; posq[3, hh] = -cc * lo
    c["posq"] = posq.astype(ml_dtypes.bfloat16)
    c["iota16"] = np.tile(np.arange(16, dtype=np.float32)[None, :], (128, 1))
    c["zeros"] = np.zeros((128, 512), ml_dtypes.bfloat16)
    return c


def build_program(S, L, do_attn=True, do_peer=True, dbg=False, stop=None):
    NT = S // 128
    NQB = S // 512 if S >= 512 else 1
    QBW = min(S, 512)
    TPB = QBW // 128
    nc = bass.Bass("TRN2", target_bir_lowering=False)
    dt = nc.dram_tensor
    x_h = dt("x", [S, D], F32, kind="ExternalInput")
    w_in_h = dt("w_in", [L, D, 3072], F32, kind="ExternalInput")
    lam_h = dt("lam4", [L, 4, 64], F32, kind="ExternalInput")
    subg_h = dt("subln_g", [L, 128], F32, kind="ExternalInput")
    w_o_h = dt("w_o", [L, D, D], F32, kind="ExternalInput")
    lng_h = dt("ln_gb", [L, 4, D], F32, kind="ExternalInput")
    wq_h = dt("w_query", [L, D, 2048], F32, kind="ExternalInput")
    keys_h = dt("sub_keys", [L, 16, 128, 128], F32, kind="ExternalInput")
    eu_h = dt("expert_u", [L * NEXP, D], F32, kind="ExternalInput")
    ev_h = dt("expert_v", [L * NEXP, D], F32, kind="ExternalInput")
    c_ident = dt("c_ident", [128, 128], F32, kind="ExternalInput")
    c_tri = dt("c_tri", [128, 128], F32, kind="ExternalInput")
    c_ones = dt("c_ones", [128, 128], F32, kind="ExternalInput")
    c_maskc = dt("c_maskc", [128, 4, 512], BF16, kind="ExternalInput")
    c_masks = dt("c_masks", [128, 4, 512], BF16, kind="ExternalInput")
    c_posk = dt("c_posk", [4, S], BF16, kind="ExternalInput")
    c_posq = dt("c_posq", [4, 4, S], BF16, kind="ExternalInput")
    c_iota = dt("c_iota16", [128, 16], F32, kind="ExternalInput")
    c_zeros = dt("c_zeros", [128, 512], BF16, kind="ExternalInput")
    out_h = dt("out", [S, D], F32, kind="ExternalOutput")
    uv_h = dt("uv_scr", [L * NEXP, 2048], BF16, kind="Internal")
    dbg_h = {}
    if dbg:
        dbg_h["h1"] = dt("dbg_h1", [S, D], F32, kind="ExternalOutput")

    with ExitStack() as es:
        kb = KB(nc, es)
        op = kb.op

        def dump(name, ap, shape, dtype, reads):
            if not dbg or name in dbg_h:
                return
            dbg_h[name] = dt("dbg_" + name, list(shape), dtype, kind="ExternalOutput")
            kb.dma("sp", "dbg", dbg_h[name].ap(), ap, reads=reads)
        h = kb.sb("h", [128, NT, D], F32)
        ident = kb.sb("ident", [128, 128], F32)
        tri = kb.sb("tri", [128, 128], F32)
        ones = kb.sb("ones", [128, 128], F32)
        onesb = kb.sb("onesb", [128, 128], BF16)
        zeros = kb.sb("zeros", [128, 512], BF16)
        iota16 = kb.sb("iota16", [128, 16], F32)
        pb = [kb.ps(f"pb{i}", [128, 512], F32) for i in range(8)]
        ht = [Buf(f"h{t}") for t in range(NT)]

        kb.dma("sp", "c0", ident.t[:], c_ident.ap(), writes=[ident])
        kb.dma("sp", "c0", tri.t[:], c_tri.ap(), writes=[tri])
        kb.dma("sp", "c0", ones.t[:], c_ones.ap(), writes=[ones])
        kb.dma("sp", "c0", zeros.t[:], c_zeros.ap(), writes=[zeros])
        kb.dma("sp", "c0", iota16.t[:], c_iota.ap(), writes=[iota16])
        xv = x_h.ap().rearrange("(n p) d -> p n d", p=128)
        for t in range(NT):
            kb.dma("sp", "xin", h.t[:, t, :], xv[:, t, :], writes=[ht[t]])
        op("pool", lambda e: e.tensor_copy(out=onesb.t[:], in_=ones.t[:]), reads=[ones], writes=[onesb])
        identb = kb.sb("identb", [128, 128], BF16)
        op("pool", lambda e: e.tensor_copy(out=identb.t[:], in_=ident.t[:]), reads=[ident], writes=[identb])

        def build_hT(hT):
            for t in range(NT):
                for half in range(2):
                    bank = pb[(2 * t + half) % 4]
                    for c4 in range(4):
                        c = half * 4 + c4
                        op("pe", lambda e: e.transpose(out=bank.t[:, c4 * 128:(c4 + 1) * 128], in_=h.t[:, t, c * 128:(c + 1) * 128],
                                                       identity=ident.t[:]), reads=[ht[t], ident], writes=[bank])
                    op("act", lambda e: e.activation(out=hT.t[:, half * 4:half * 4 + 4, t * 128:(t + 1) * 128],
                                                     in_=bank.t[:].rearrange("p (c n) -> p c n", c=4), func=AF.Copy),
                       reads=[bank], writes=[hT])

        def layer_norm_tile(t, src, gt, bt_, tmp, st, mv, r2):
            for c in range(2):
                op("dve", lambda e: e.bn_stats(out=st.t[:, c, :], in_=src[:, c * 512:(c + 1) * 512]), reads=[tmp, ht[t]], writes=[st])
            op("dve", lambda e: e.bn_aggr(out=mv.t[:], in_=st.t[:].rearrange("p a b -> p (a b)")), reads=[st], writes=[mv])
            op("act", lambda e: e.activation(out=r2.t[:, 0:1], in_=mv.t[:, 1:2], func=AF.Ln, bias=epsb.t[:, 0:1], scale=1.0), reads=[mv, epsb], writes=[r2])
            op("act", lambda e: e.activation(out=r2.t[:, 1:2], in_=r2.t[:, 0:1], func=AF.Exp, scale=-0.5), reads=[r2], writes=[r2])
            op("dve", lambda e: e.tensor_scalar(out=tmp.t[:], in0=src, scalar1=mv.t[:, 0:1], scalar2=r2.t[:, 1:2],
                                                op0=ALU.subtract, op1=ALU.mult), reads=[tmp, ht[t]], sreads=[mv, r2], writes=[tmp])
            op("dve", lambda e: e.tensor_tensor(out=tmp.t[:], in0=tmp.t[:], in1=gt.t[:], op=ALU.mult), reads=[tmp, gt], writes=[tmp])
            op("dve", lambda e: e.tensor_tensor(out=h.t[:, t, :], in0=tmp.t[:], in1=bt_.t[:], op=ALU.add), reads=[tmp, bt_], writes=[ht[t]])

        scr = [Buf(f"scr{i}") for i in range(L)]

        def convert_layer(cl):
            CH = 2048
            d = kb.dsem(f"cv{cl}")
            for c in range(NEXP // CH):
                r0 = cl * NEXP + c * CH
                if d.count >= 16 * 4:
                    kb.E["pool"].eng.wait_ge(d.sem, d.count - 16 * 3)
                kb.dma("pool", f"cv{cl}", uv_h[r0:r0 + CH, 0:D], eu_h[r0:r0 + CH, :], writes=[scr[cl]])
                kb.dma("pool", f"cv{cl}", uv_h[r0:r0 + CH, D:2 * D], ev_h[r0:r0 + CH, :], writes=[scr[cl]])

        epsb = kb.sb("epsb", [128, 1], F32)
        op("pool", lambda e: e.memset(epsb.t[:], LN_EPS), writes=[epsb])

        for l in range(L):
            li = 0.8 - 0.6 * math.exp(-0.3 * l)
            if do_attn:
                with ExitStack() as sa:
                    hT = kb.sb("hT", [128, 8, S], BF16, sa)
                    build_hT(hT)
                    wst = kb.sb("wst", [128, 8, 384], F32, sa)
                    wbf = kb.sb("wbf", [128, 8, 384], BF16, sa)
                    wost = kb.sb("wost", [128, D], F32, sa)
                    wobf = kb.sb("wobf", [128, D], BF16, sa)
                    QT = kb.sb("QT", [128, S], BF16, sa)
                    KT = kb.sb("KT", [128, S], BF16, sa)
                    V = kb.sb("V", [128, NT, 128], BF16, sa)
                    mixT = kb.sb("mixT", [128, S], BF16, sa)
                    maskc = kb.sb("maskc", [128, 4, 512], BF16, sa)
                    masks = kb.sb("masks", [128, 4, 512], BF16, sa)
                    posk = kb.sb("posk", [4, S], BF16, sa)
                    posq = kb.sb("posq", [4, S], BF16, sa)
                    NEB = 6; NEZ = 5; NSP = 4; NE2 = 3
                    Eb = [kb.sb(f"E{i}", [128, 512], BF16, sa) for i in range(NEB)]
                    ez = [kb.sb(f"ez{i}", [128, 512], F32, sa) for i in range(NEZ)]
                    spb = [kb.sb(f"sp{i}", [128, 512], BF16, sa) for i in range(NSP)]
                    Rbs = [kb.sb(f"Rb{i}", [128, 512], BF16, sa) for i in range(2)]
                    trib = kb.sb("trib", [128, 128], BF16, sa)
                    ZbA = Buf("ZbA"); ZbB = Buf("ZbB")
                    e2 = [kb.sb(f"e2{i}", [128, 512], F32, sa) for i in range(NE2)]
                    R = kb.sb("R", [128, 512], F32, sa)
                    lamt = kb.sb("lamt", [128, 4, 64], F32, sa)
                    lamp = kb.sb("lamp", [128, 2, 64], F32, sa)
                    lams = kb.sb("lams", [128, 4], F32, sa)
                    gsc = kb.sb("gsc", [128, 128], F32, sa)
                    rz = kb.sb("rz", [128, 8], F32, sa)
                    ss = kb.sb("ss", [128, 4], F32, sa)
                    rstd = kb.sb("rstd", [128, 4], F32, sa)
                    tq = [kb.sb(f"tq{i}", [128, 128], F32, sa) for i in range(2)]
                    dq_ = [kb.sb(f"dq{i}", [128, 128], F32, sa) for i in range(4)]
                    dn = [kb.sb(f"dn{i}", [128, 128], F32, sa) for i in range(2)]
                    junk = kb.sb("junk", [128, 128], F32, sa)
                    osb = kb.sb("osb", [128, 512], F32, sa)
                    g1 = kb.sb("g1", [128, D], F32, sa)
                    b1 = kb.sb("b1", [128, D], F32, sa)
                    lntmp = kb.sb("lntmp", [128, D], F32, sa)
                    st = kb.sb("st", [128, 2, 6], F32, sa)
                    mv = kb.sb("mv", [128, 2], F32, sa)
                    r2 = kb.sb("r2", [128, 2], F32, sa)
                    rmseps = kb.sb("rmseps", [128, 1], F32, sa)

                    kb.dma("sp", "c1", maskc.t[:], c_maskc.ap(), writes=[maskc])
                    kb.dma("sp", "c1", masks.t[:], c_masks.ap(), writes=[masks])
                    kb.dma("sp", "c1", posk.t[:], c_posk.ap(), writes=[posk])
                    kb.dma("sp", "c1", lamt.t[:].rearrange("p a b -> p (a b)"),
                           bc(lam_h[l:l + 1, :, :].rearrange("o a b -> o (a b)"), [128, 256]), writes=[lamt])
                    kb.dma("sp", "c1", gsc.t[:], bc(subg_h[l:l + 1, :], [128, 128]), writes=[gsc])
                    kb.dma("sp", "c1", g1.t[:], bc(lng_h[l, 0:1, :], [128, D]), writes=[g1])
                    kb.dma("sp", "c1", b1.t[:], bc(lng_h[l, 1:2, :], [128, D]), writes=[b1])
                    op("pool", lambda e: e.memset(rmseps.t[:], RMS_EPS), writes=[rmseps])
                    op("pool", lambda e: e.tensor_copy(out=trib.t[:], in_=tri.t[:]), reads=[tri], writes=[trib])
                    op("dve", lambda e: e.tensor_tensor(out=lamp.t[:, 0, :], in0=lamt.t[:, 0, :], in1=lamt.t[:, 1, :], op=ALU.mult), reads=[lamt], writes=[lamp])
                    op("dve", lambda e: e.tensor_tensor(out=lamp.t[:, 1, :], in0=lamt.t[:, 2, :], in1=lamt.t[:, 3, :], op=ALU.mult), reads=[lamt], writes=[lamp])
                    op("dve", lambda e: e.tensor_reduce(out=lams.t[:, 0:2], in_=lamp.t[:], axis=AX.X, op=ALU.add), reads=[lamp], writes=[lams])
                    op("act", lambda e: e.activation(out=lams.t[:, 0:2], in_=lams.t[:, 0:2], func=AF.Exp), reads=[lams], writes=[lams])
                    op("dve", lambda e: e.tensor_tensor(out=lams.t[:, 2:3], in0=lams.t[:, 1:2], in1=lams.t[:, 0:1], op=ALU.subtract), reads=[lams], writes=[lams])
                    op("dve", lambda e: e.tensor_scalar(out=lams.t[:, 3:4], in0=lams.t[:, 2:3], scalar1=-li, scalar2=None, op0=ALU.add), reads=[lams], writes=[lams])
                    mlam = lams.t[:, 3:4]
                    op("act", lambda e: e.activation(out=gsc.t[:], in_=gsc.t[:], func=AF.Copy, scale=(1.0 - li)), reads=[gsc], writes=[gsc])

                    for g in range(8):
                        is_diff = g < 4
                        if is_diff:
                            cq, ck, cv = g * 128, 512 + g * 128, 1024 + g * 128
                        else:
                            cq, ck, cv = 1536 + (g - 4) * 128, 2048 + (g - 4) * 128, 2560 + (g - 4) * 128
                        wv = w_in_h[l].rearrange("(c p) n -> p c n", p=128)
                        for j, c0 in enumerate((cq, ck, cv)):
                            kb.dma("sp", "wst", wst.t[:, :, j * 128:(j + 1) * 128], wv[:, :, c0:c0 + 128], writes=[wst])
                        kb.dma("sp", "wost", wost.t[:], w_o_h[l, g * 128:(g + 1) * 128, :], writes=[wost])
                        op("dve", lambda e: e.tensor_copy(out=wbf.t[:], in_=wst.t[:]), reads=[wst], writes=[wbf])
                        op("dve", lambda e: e.tensor_copy(out=wobf.t[:], in_=wost.t[:]), reads=[wost], writes=[wobf])
                        if g == 0:
                            convert_layer(l)
                        if is_diff:
                            kb.dma("sp", "posq", posq.t[:, :], c_posq[:, g, :], writes=[posq])
                        for j, dst in ((0, QT), (1, KT)):
                            for blk in range(NQB):
                                bank = pb[(2 * j + blk) % 4]
                                for c in range(8):
                                    op("pe", lambda e: e.matmul(out=bank.t[:, 0:QBW], lhsT=wbf.t[:, c, j * 128:(j + 1) * 128],
                                                                rhs=hT.t[:, c, blk * QBW:(blk + 1) * QBW], start=(c == 0), stop=(c == 7)),
                                       reads=[wbf, hT], writes=[bank])
                                op("act", lambda e: e.activation(out=dst.t[:, blk * QBW:(blk + 1) * QBW], in_=bank.t[:, 0:QBW], func=AF.Copy),
                                   reads=[bank], writes=[dst])
                        for t4 in range(0, NT, 4):
                            bank = pb[(t4 // 4) % 4]
                            nt4 = min(4, NT - t4)
                            for tt in range(nt4):
                                t = t4 + tt
                                for c in range(8):
                                    op("pe", lambda e: e.matmul(out=bank.t[:, tt * 128:(tt + 1) * 128], lhsT=hT.t[:, c, t * 128:(t + 1) * 128],
                                                                rhs=wbf.t[:, c, 256:384], start=(c == 0), stop=(c == 7)),
                                       reads=[wbf, hT], writes=[bank])
                            op("act", lambda e: e.activation(out=V.t[:, t4:t4 + nt4, :], in_=bank.t[:, 0:nt4 * 128].rearrange("p (a b) -> p a b", b=128),
                                                             func=AF.Copy), reads=[bank], writes=[V])
                        steps = []
                        if is_diff:
                            for b in range(NQB):
                                q0 = b * QBW
                                kts = list(range(b * TPB + TPB))
                                if b % 2 == 0:
                                    O = [pb[4], pb[5]]; Zb = ZbA; zc = 0
                                else:
                                    O = [pb[2], pb[3]]; Zb = ZbB; zc = 8
                                for m in range(2):
                                    for ki, kt in enumerate(kts):
                                        first = (m == 0 and ki == 0); last = (m == 1 and ki == len(kts) - 1)

                                        def s1(b=b, q0=q0, m=m, kt=kt, sidx=len(steps)):
                                            pr = slice(m * 64, (m + 1) * 64)
                                            j = kt - b * TPB
                                            sbank = pb[sidx % 2]; E = Eb[sidx % NEB]
                                            op("pe", lambda e: e.matmul(out=sbank.t[:, 0:QBW], lhsT=KT.t[pr, kt * 128:(kt + 1) * 128],
                                                                        rhs=QT.t[pr, q0:q0 + QBW], start=True, stop=False),
                                               reads=[KT, QT], writes=[sbank])
                                            op("pe", lambda e: e.matmul(out=sbank.t[:, 0:QBW], lhsT=posk.t[0:4, kt * 128:(kt + 1) * 128],
                                                                        rhs=posq.t[0:4, q0:q0 + QBW], start=False, stop=True),
                                               reads=[posk, posq], writes=[sbank])
                                            op("act", lambda e: e.activation(out=E.t[:, 0:QBW], in_=sbank.t[:, 0:QBW], func=AF.Exp, scale=SCALE),
                                               reads=[sbank], writes=[E])
                                            if j >= 0:
                                                op("dve", lambda e: e.tensor_tensor(out=E.t[:, 0:QBW], in0=E.t[:, 0:QBW], in1=maskc.t[:, j, 0:QBW], op=ALU.mult),
                                                   reads=[E, maskc], writes=[E])

                                        def s2(b=b, q0=q0, m=m, kt=kt, sidx=len(steps), first=first, last=last, O=O, Zb=Zb, zc=zc, kts=kts):
                                            j = kt - b * TPB
                                            E = Eb[sidx % NEB]
                                            if first:
                                                for mm in range(2):
                                                    op("pe", lambda e: e.matmul(out=O[mm].t[:, :], lhsT=zeros.t[:, 0:128], rhs=zeros.t[:, :], start=True, stop=False),
                                                       reads=[zeros], writes=[O[mm]])
                                                op("pe", lambda e: e.matmul(out=pb[6].t[:, zc:zc + 8], lhsT=zeros.t[:, 0:128], rhs=zeros.t[:, 0:8], start=True, stop=False),
                                                   reads=[zeros], writes=[Zb])
                                            for tt in range(max(j, 0), TPB):
                                                op("pe", lambda e: e.matmul(out=O[m].t[:, tt * 128:(tt + 1) * 128], lhsT=E.t[:, tt * 128:(tt + 1) * 128],
                                                                            rhs=V.t[:, kt, :], start=False, stop=(kt == kts[-1]), skip_group_check=True),
                                                   reads=[E, V], writes=[O[m]])
                                                op("pe", lambda e: e.matmul(out=pb[6].t[:, zc + m * 4 + tt:zc + m * 4 + tt + 1], lhsT=E.t[:, tt * 128:(tt + 1) * 128],
                                                                            rhs=onesb.t[:, 0:1], start=False, stop=(kt == kts[-1]), skip_group_check=True),
                                                   reads=[E, onesb], writes=[Zb])
                                            if not last:
                                                return
                                            op("dve", lambda e: e.reciprocal(out=rz.t[:], in_=pb[6].t[:, zc:zc + 8]), reads=[Zb], writes=[rz])
                                            op("dve", lambda e: e.tensor_scalar(out=rz.t[:, 4:8], in0=rz.t[:, 4:8], scalar1=mlam, scalar2=None, op0=ALU.mult),
                                               reads=[rz], sreads=[lams], writes=[rz])
                                            for tt in range(TPB):
                                                tqb = tq[tt % 2]; dd = dq_[tt]
                                                op("act", lambda e: e.activation(out=tqb.t[:], in_=O[0].t[:, tt * 128:(tt + 1) * 128], func=AF.Copy, scale=rz.t[:, tt:tt + 1]),
                                                   reads=[O[0]], sreads=[rz], writes=[tqb])
                                                op("dve", lambda e: e.scalar_tensor_tensor(out=dd.t[:], in0=O[1].t[:, tt * 128:(tt + 1) * 128], scalar=rz.t[:, 4 + tt:5 + tt],
                                                                                           in1=tqb.t[:], op0=ALU.mult, op1=ALU.add), reads=[O[1], tqb], sreads=[rz], writes=[dd])
                                                op("act", lambda e: e.activation(out=junk.t[:], in_=dd.t[:], func=AF.Square, accum_out=ss.t[:, tt:tt + 1]),
                                                   reads=[dd], writes=[junk, ss])
                                            op("act", lambda e: e.activation(out=rstd.t[:, 0:TPB], in_=ss.t[:, 0:TPB], func=AF.Ln, scale=1.0 / 128, bias=rmseps.t[:, 0:1]),
                                               reads=[ss, rmseps], writes=[rstd])
                                            op("act", lambda e: e.activation(out=rstd.t[:, 0:TPB], in_=rstd.t[:, 0:TPB], func=AF.Exp, scale=-0.5), reads=[rstd], writes=[rstd])
                                            tb = pb[7]
                                            for tt in range(TPB):
                                                dnb = dn[tt % 2]
                                                op("dve", lambda e: e.scalar_tensor_tensor(out=dnb.t[:], in0=dq_[tt].t[:], scalar=rstd.t[:, tt:tt + 1], in1=gsc.t[:],
                                                                                           op0=ALU.mult, op1=ALU.mult), reads=[dq_[tt], gsc], sreads=[rstd], writes=[dnb])
                                                op("pe", lambda e: e.transpose(out=tb.t[:, tt * 128:(tt + 1) * 128], in_=dnb.t[:], identity=ident.t[:]),
                                                   reads=[dnb, ident], writes=[tb])
                                            op("act", lambda e: e.activation(out=mixT.t[:, q0:q0 + QBW], in_=tb.t[:, 0:QBW], func=AF.Copy), reads=[tb], writes=[mixT])
                                        steps.append((s1, s2))
                            SKEW = (0, 3)
                        else:
                            for b in range(NQB):
                                q0 = b * QBW
                                kts = list(range(b * TPB + TPB))
                                O = pb[4 + b % 2]
                                for p in range(2):
                                    for si, kt in enumerate(reversed(kts)):
                                        first = (p == 0 and si == 0); last = (p == 1 and kt == 0)

                                        def s1(b=b, q0=q0, p=p, kt=kt, sidx=len(steps)):
                                            pr = slice(p * 64, (p + 1) * 64)
                                            j = kt - b * TPB
                                            zbank = pb[sidx % 2]; ezb = ez[sidx % NEZ]; sp_ = spb[sidx % NSP]
                                            op("pe", lambda e: e.matmul(out=zbank.t[:, 0:QBW], lhsT=KT.t[pr, kt * 128:(kt + 1) * 128],
                                                                        rhs=QT.t[pr, q0:q0 + QBW], start=True, stop=(j < 0)), reads=[KT, QT], writes=[zbank])
                                            if j >= 0:
                                                op("pe", lambda e: e.matmul(out=zbank.t[:, 0:QBW], lhsT=identb.t[:], rhs=masks.t[:, j, 0:QBW], start=False, stop=True),
                                                   reads=[identb, masks], writes=[zbank])
                                            op("act", lambda e: e.activation(out=ezb.t[:, 0:QBW], in_=zbank.t[:, 0:QBW], func=AF.Exp, scale=SCALE),
                                               reads=[zbank], writes=[ezb])
                                            op("act", lambda e: e.activation(out=sp_.t[:, 0:QBW], in_=ezb.t[:, 0:QBW], func=AF.Ln, bias=ones.t[:, 0:1], scale=1.0),
                                               reads=[ezb, ones], writes=[sp_])

                                        def s2(b=b, kt=kt, si=si, sidx=len(steps)):
                                            cbank = pb[2 + sidx % 2]; sp_ = spb[sidx % NSP]; e2b = e2[sidx % NE2]
                                            Rb = Rbs[(sidx + 1) % 2]
                                            Rbn = Rbs[sidx % 2]
                                            op("pe", lambda e: e.matmul(out=cbank.t[:, 0:QBW], lhsT=trib.t[:], rhs=sp_.t[:, 0:QBW], start=True, stop=(si == 0)),
                                               reads=[trib, sp_], writes=[cbank])
                                            if si > 0:
                                                op("pe", lambda e: e.matmul(out=cbank.t[:, 0:QBW], lhsT=onesb.t[:], rhs=Rb.t[:, 0:QBW], start=False, stop=True),
                                                   reads=[onesb, Rb], writes=[cbank])
                                            if si == 0:
                                                op("dve", lambda e: e.tensor_copy(out=R.t[:, 0:QBW], in_=sp_.t[:, 0:QBW]), reads=[sp_], writes=[R])
                                                if kt > 0:
                                                    op("dve", lambda e: e.tensor_copy(out=Rbn.t[:, 0:QBW], in_=sp_.t[:, 0:QBW]), reads=[sp_], writes=[Rbn])
                                            elif kt > 0:
                                                op("dve", lambda e: e.tensor_tensor(out=R.t[:, 0:QBW], in0=R.t[:, 0:QBW], in1=sp_.t[:, 0:QBW], op=ALU.add),
                                                   reads=[R, sp_], writes=[R])
                                                op("dve", lambda e: e.tensor_copy(out=Rbn.t[:, 0:QBW], in_=R.t[:, 0:QBW]), reads=[R], writes=[Rbn])
                                            op("act", lambda e: e.activation(out=e2b.t[:, 0:QBW], in_=cbank.t[:, 0:QBW], func=AF.Exp, scale=-1.0),
                                               reads=[cbank], writes=[e2b])

                                        def s3(b=b, q0=q0, p=p, kt=kt, sidx=len(steps), first=first, last=last, O=O):
                                            j = kt - b * TPB
                                            ezb = ez[sidx % NEZ]; e2b = e2[sidx % NE2]; W = Eb[sidx % NEB]
                                            op("dve", lambda e: e.tensor_tensor(out=W.t[:, 0:QBW], in0=ezb.t[:, 0:QBW], in1=e2b.t[:, 0:QBW], op=ALU.mult),
                                               reads=[ezb, e2b], writes=[W])
                                            if first:
                                                op("pe", lambda e: e.matmul(out=O.t[:, :], lhsT=zeros.t[:, 0:128], rhs=zeros.t[:, :], start=True, stop=False),
                                                   reads=[zeros], writes=[O])
                                            for tt in range(max(j, 0), TPB):
                                                op("pe", lambda e: e.matmul(out=O.t[:, tt * 128 + p * 64:tt * 128 + (p + 1) * 64], lhsT=W.t[:, tt * 128:(tt + 1) * 128],
                                                                            rhs=V.t[:, kt, p * 64:(p + 1) * 64], start=False, stop=(kt == 0), skip_group_check=True),
                                                   reads=[W, V], writes=[O])
                                            if not last:
                                                return
                                            op("act", lambda e: e.activation(out=osb.t[:, 0:QBW], in_=O.t[:, 0:QBW], func=AF.Copy), reads=[O], writes=[osb])
                                            tb = pb[7]
                                            for tt in range(TPB):
                                                op("pe", lambda e: e.transpose(out=tb.t[:, tt * 128:(tt + 1) * 128], in_=osb.t[:, tt * 128:(tt + 1) * 128], identity=ident.t[:]),
                                                   reads=[osb, ident], writes=[tb])
                                            op("act", lambda e: e.activation(out=mixT.t[:, q0:q0 + QBW], in_=tb.t[:, 0:QBW], func=AF.Copy), reads=[tb], writes=[mixT])
                                        steps.append((s1, s2, s3))
                            SKEW = (0, 2, 4)
                        for n in range(len(steps) + max(SKEW)):
                            for k in reversed(range(len(SKEW))):
                                i = n - SKEW[k]
                                if 0 <= i < len(steps):
                                    steps[i][k]()
                        for t in range(NT):
                            for half in range(2):
                                bank = pb[2 + (2 * t + half) % 2]
                                op("pe", lambda e: e.matmul(out=bank.t[:, :], lhsT=mixT.t[:, t * 128:(t + 1) * 128], rhs=wobf.t[:, half * 512:(half + 1) * 512],
                                                            start=True, stop=True), reads=[mixT, wobf], writes=[bank])
                                hs = h.t[:, t, half * 512:(half + 1) * 512]
                                if g == 0:
                                    op("dve", lambda e: e.scalar_tensor_tensor(out=hs, in0=hs, scalar=ALPHA, in1=bank.t[:, :], op0=ALU.mult, op1=ALU.add),
                                       reads=[bank, ht[t]], writes=[ht[t]])
                                else:
                                    op("dve", lambda e: e.tensor_tensor(out=hs, in0=hs, in1=bank.t[:, :], op=ALU.add), reads=[bank, ht[t]], writes=[ht[t]])
                    for t in range(NT):
                        layer_norm_tile(t, h.t[:, t, :], g1, b1, lntmp, st, mv, r2)
                    kb.barrier()
            if dbg and l == 0:
                hv = dbg_h["h1"].ap().rearrange("(n p) d -> p n d", p=128)
                for t in range(NT):
                    kb.dma("sp", "dbg", hv[:, t, :], h.t[:, t, :], reads=[ht[t]])
            if do_peer:
                if not do_attn:
                    convert_layer(l)
                with ExitStack() as sp1:
                    top_s = kb.sb("top_s", [128, NT, 16, 16], F32, sp1)
                    top_i = kb.sb("top_i", [128, NT, 16, 16], U16, sp1)
                    with ExitStack() as sq:
                        hT = kb.sb("hT", [128, 8, S], BF16, sq)
                        build_hT(hT)
                        kst = kb.sb("kst", [128, 16, 128], F32, sq)
                        keysT = kb.sb("keysT", [128, 16, 128], BF16, sq)
                        wqst = [kb.sb(f"wqst{i}", [128, 8, 128], F32, sq) for i in range(2)]
                        wqbf = [kb.sb(f"wqbf{i}", [128, 8, 128], BF16, sq) for i in range(2)]
                        qT = [kb.sb(f"qT{i}", [128, S], BF16, sq) for i in range(2)]
                        sc = [kb.sb(f"sc{i}", [128, S], F32, sq) for i in range(2)]
                        mrs = [kb.sb(f"mr{i}", [128, 128], F32, sq) for i in range(4)]
                        tsb = [Buf(f"tsb{i}") for i in range(NT)]; tib = [Buf(f"tib{i}") for i in range(NT)]
                        kb.dma("sp", "kst", kst.t[:], keys_h[l].rearrange("a n c -> n a c"), writes=[kst])
                        for hp in range(16):
                            bank = pb[hp % 4]
                            op("pe", lambda e: e.transpose(out=bank.t[:, 0:128], in_=kst.t[:, hp, :], identity=ident.t[:]), reads=[kst, ident], writes=[bank])
                            op("act", lambda e: e.activation(out=keysT.t[:, hp, :], in_=bank.t[:, 0:128], func=AF.Copy), reads=[bank], writes=[keysT])
                        wqv = wq_h[l].rearrange("(c p) n -> p c n", p=128)
                        for hp in range(16):
                            ws = wqst[hp % 2]; wb = wqbf[hp % 2]; qt_ = qT[hp % 2]; scb = sc[hp % 2]
                            kb.dma("sp", f"wq{hp % 2}", ws.t[:], wqv[:, :, hp * 128:(hp + 1) * 128], writes=[ws])
                            op("pool", lambda e: e.tensor_copy(out=wb.t[:], in_=ws.t[:]), reads=[ws], writes=[wb])
                            for blk in range(NQB):
                                bank = pb[blk % 4]
                                for c in range(8):
                                    op("pe", lambda e: e.matmul(out=bank.t[:, 0:QBW], lhsT=wb.t[:, c, :], rhs=hT.t[:, c, blk * QBW:(blk + 1) * QBW],
                                                                start=(c == 0), stop=(c == 7)), reads=[wb, hT], writes=[bank])
                                op("act", lambda e: e.activation(out=qt_.t[:, blk * QBW:(blk + 1) * QBW], in_=bank.t[:, 0:QBW], func=AF.Copy),
                                   reads=[bank], writes=[qt_])
                            for blk in range(NQB):
                                bank = pb[4 + blk % 4]
                                for tt in range(TPB):
                                    t = blk * TPB + tt
                                    op("pe", lambda e: e.matmul(out=bank.t[:, tt * 128:(tt + 1) * 128], lhsT=qt_.t[:, t * 128:(t + 1) * 128], rhs=keysT.t[:, hp, :],
                                                                start=True, stop=True), reads=[qt_, keysT], writes=[bank])
                                op("act", lambda e: e.activation(out=scb.t[:, blk * QBW:(blk + 1) * QBW], in_=bank.t[:, 0:QBW], func=AF.Copy),
                                   reads=[bank], writes=[scb])
                            TG = 4
                            for t0_ in range(0, NT, TG):
                                tl = list(range(t0_, min(NT, t0_ + TG)))
                                sv = {t: scb.t[:, t * 128:(t + 1) * 128] for t in tl}
                                for t in tl:
                                    op("dve", lambda e: e.max(out=top_s.t[:, t, hp, 0:8], in_=sv[t]), reads=[scb], writes=[tsb[t]])
                                for t in tl:
                                    op("dve", lambda e: e.tensor_scalar(out=mrs[t % TG].t[:], in0=sv[t], scalar1=top_s.t[:, t, hp, 7:8], scalar2=None, op0=ALU.is_ge),
                                       reads=[scb], sreads=[tsb[t]], writes=[mrs[t % TG]])
                                for t in tl:
                                    op("dve", lambda e: e.scalar_tensor_tensor(out=mrs[t % TG].t[:], in0=mrs[t % TG].t[:], scalar=-1e30, in1=sv[t], op0=ALU.mult, op1=ALU.add),
                                       reads=[scb, mrs[t % TG]], writes=[mrs[t % TG]])
                                for t in tl:
                                    op("dve", lambda e: e.max(out=top_s.t[:, t, hp, 8:16], in_=mrs[t % TG].t[:]), reads=[mrs[t % TG]], writes=[tsb[t]])
                                for t in tl:
                                    op("dve", lambda e: e.max_index(out=top_i.t[:, t, hp, 0:8], in_max=top_s.t[:, t, hp, 0:8], in_values=sv[t]),
                                       reads=[scb, tsb[t]], writes=[tib[t]])
                                for t in tl:
                                    op("dve", lambda e: e.max_index(out=top_i.t[:, t, hp, 8:16], in_max=top_s.t[:, t, hp, 8:16], in_values=sv[t]),
                                       reads=[scb, tsb[t]], writes=[tib[t]])
                        for t in range(NT):
                            top_s.w = tsb[t].w if top_s.w is None or (tsb[t].w and tsb[t].w[1] > top_s.w[1]) else top_s.w
                            top_i.w = tib[t].w if top_i.w is None or (tib[t].w and tib[t].w[1] > top_i.w[1]) else top_i.w
                        kb.barrier()
                    with ExitStack() as sg:
                        tif = kb.sb("tif", [128, 16, 16], F32, sg)
                        cand = kb.sb("cand", [128, 8, 16, 16], F32, sg)
                        mr2 = kb.sb("mr2", [128, 256], F32, sg)
                        bs = kb.sb("bs", [128, 8, 16], F32, sg)
                        bp = kb.sb("bp", [128, 8, 16], U32, sg)
                        ab_i = kb.sb("ab_i", [128, 2, 128], U32, sg)
                        ab_f = kb.sb("ab_f", [128, 2, 8, 16], F32, sg)
                        oh = kb.sb("oh", [128, 8, 16, 16], F32, sg)
                        sel = kb.sb("sel", [128, 2, 128], F32, sg)
                        idxf = kb.sb("idxf", [128, 128], F32, sg)
                        idx = [kb.sb(f"idx{i}", [128, 128], I32, sg) for i in range(2)]
                        gate = [kb.sb(f"gate{i}", [128, 8, 16], F32, sg) for i in range(2)]
                        gsum = kb.sb("gsum", [128, 8], F32, sg)
                        act = [kb.sb(f"act{i}", [128, 128], F32, sg) for i in range(2)]
                        ga = [kb.sb(f"ga{i}", [128, 128], F32, sg) for i in range(2)]
                        gb_ = [kb.sb(f"gb{i}", [128, 2048], BF16, sg) for i in range(NB_G)]
                        prod = [kb.sb(f"prod{i}", [128, D], BF16, sg) for i in range(2)]
                        xbf = [kb.sb(f"xbf{i}", [128, D], BF16, sg) for i in range(2)]
                        y = kb.sb("y", [128, D], F32, sg)
                        junkb = kb.sb("junkb", [128, D], BF16, sg)
                        g2 = kb.sb("g2", [128, D], F32, sg)
                        b2 = kb.sb("b2", [128, D], F32, sg)
                        st = kb.sb("st", [128, 2, 6], F32, sg)
                        mv = kb.sb("mv", [128, 2], F32, sg)
                        r2 = kb.sb("r2", [128, 2], F32, sg)
                        kb.dma("sp", "c2", g2.t[:], bc(lng_h[l, 2:3, :], [128, D]), writes=[g2])
                        kb.dma("sp", "c2", b2.t[:], bc(lng_h[l, 3:4, :], [128, D]), writes=[b2])

                        def p2_ops(t):
                            ix = idx[t % 2]; gt_ = gate[t % 2]
                            ts4 = top_s.t[:, t, :, :].rearrange("p (h two) k -> p h two k", two=2)
                            tif4 = tif.t[:].rearrange("p (h two) k -> p h two k", two=2)
                            T = []
                            T.append(lambda: op("dve", lambda e: e.tensor_copy(out=tif.t[:], in_=top_i.t[:, t, :, :]), reads=[top_i], writes=[tif]))
                            T.append(lambda: op("dve", lambda e: e.tensor_tensor(out=cand.t[:], in0=bc(ts4[:, :, 0, :].unsqueeze(3), [128, 8, 16, 16]),
                                                                                 in1=bc(ts4[:, :, 1, :].unsqueeze(2), [128, 8, 16, 16]), op=ALU.add),
                                                reads=[top_s], writes=[cand]))
                            for hh in range(8):
                                def f(hh=hh):
                                    cv_ = cand.t[:, hh, :, :].rearrange("p a b -> p (a b)")
                                    op("dve", lambda e: e.max(out=bs.t[:, hh, 0:8], in_=cv_), reads=[cand], writes=[bs])
                                    op("dve", lambda e: e.tensor_scalar(out=mr2.t[:], in0=cv_, scalar1=bs.t[:, hh, 7:8], scalar2=None, op0=ALU.is_ge),
                                       reads=[cand], sreads=[bs], writes=[mr2])
                                    op("dve", lambda e: e.scalar_tensor_tensor(out=mr2.t[:], in0=mr2.t[:], scalar=-1e30, in1=cv_, op0=ALU.mult, op1=ALU.add),
                                       reads=[cand, mr2], writes=[mr2])
                                    op("dve", lambda e: e.max(out=bs.t[:, hh, 8:16], in_=mr2.t[:]), reads=[mr2], writes=[bs])
                                    op("dve", lambda e: e.max_index(out=bp.t[:, hh, 0:8], in_max=bs.t[:, hh, 0:8], in_values=cv_), reads=[cand, bs], writes=[bp])
                                    op("dve", lambda e: e.max_index(out=bp.t[:, hh, 8:16], in_max=bs.t[:, hh, 8:16], in_values=cv_), reads=[cand, bs], writes=[bp])
                                T.append(f)
                            bpf = bp.t[:].rearrange("p h k -> p (h k)")
                            T.append(lambda: op("dve", lambda e: e.tensor_single_scalar(out=ab_i.t[:, 0, :], in_=bpf, scalar=4, op=ALU.logical_shift_right), reads=[bp], writes=[ab_i]))
                            T.append(lambda: op("dve", lambda e: e.tensor_single_scalar(out=ab_i.t[:, 1, :], in_=bpf, scalar=15, op=ALU.bitwise_and), reads=[bp], writes=[ab_i]))
                            T.append(lambda: op("dve", lambda e: e.tensor_copy(out=ab_f.t[:].rearrange("p a h k -> p a (h k)"), in_=ab_i.t[:]), reads=[ab_i], writes=[ab_f]))
                            for w in range(2):
                                T.append(lambda w=w: op("dve", lambda e: e.tensor_tensor(out=oh.t[:], in0=bc(ab_f.t[:, w, :, :].unsqueeze(3), [128, 8, 16, 16]),
                                                                                         in1=bc(iota16.t[:].unsqueeze(1).unsqueeze(1), [128, 8, 16, 16]), op=ALU.is_equal),
                                                        reads=[ab_f, iota16], writes=[oh]))
                                T.append(lambda w=w: op("dve", lambda e: e.tensor_tensor(out=oh.t[:], in0=oh.t[:], in1=bc(tif4[:, :, w, :].unsqueeze(2), [128, 8, 16, 16]), op=ALU.mult),
                                                        reads=[oh, tif], writes=[oh]))
                                T.append(lambda w=w: op("dve", lambda e: e.tensor_reduce(out=sel.t[:, w, :], in_=oh.t[:].rearrange("p h k a -> p (h k) a"), axis=AX.X, op=ALU.add),
                                                        reads=[oh], writes=[sel]))
                            T.append(lambda: op("dve", lambda e: e.tensor_scalar(out=idxf.t[:], in0=sel.t[:, 0, :], scalar1=128.0, scalar2=float(l * NEXP), op0=ALU.mult, op1=ALU.add),
                                                reads=[sel], writes=[idxf]))
                            T.append(lambda: op("dve", lambda e: e.tensor_tensor(out=idxf.t[:], in0=idxf.t[:], in1=sel.t[:, 1, :], op=ALU.add), reads=[idxf, sel], writes=[idxf]))
                            T.append(lambda: op("dve", lambda e: e.tensor_copy(out=ix.t[:], in_=idxf.t[:]), reads=[idxf], writes=[ix]))
                            T.append(lambda: op("dve", lambda e: e.tensor_tensor(out=gt_.t[:], in0=bs.t[:], in1=bc(bs.t[:, :, 0:1], [128, 8, 16]), op=ALU.subtract), reads=[bs], writes=[gt_]))
                            T.append(lambda: op("act", lambda e: e.activation(out=gt_.t[:], in_=gt_.t[:], func=AF.Exp), reads=[gt_], writes=[gt_]))
                            T.append(lambda: op("dve", lambda e: e.tensor_reduce(out=gsum.t[:], in_=gt_.t[:], axis=AX.X, op=ALU.add), reads=[gt_], writes=[gsum]))
                            T.append(lambda: op("dve", lambda e: e.reciprocal(out=gsum.t[:], in_=gsum.t[:]), reads=[gsum], writes=[gsum]))
                            T.append(lambda: op("dve", lambda e: e.tensor_tensor(out=gt_.t[:], in0=gt_.t[:], in1=bc(gsum.t[:].unsqueeze(2), [128, 8, 16]), op=ALU.mult),
                                                reads=[gt_, gsum], writes=[gt_]))
                            if t == 0:
                                T.append(lambda: (dump("top_s", top_s.t[:, 0, :, :], [128, 16, 16], F32, [top_s]),
                                                  dump("idx", ix.t[:], [128, 128], I32, [ix]),
                                                  dump("gate", gt_.t[:], [128, 8, 16], F32, [gt_])))
                            return T

                        for th in p2_ops(0):
                            th()
                        gcount = 0; pcount = 0; dcount = 0
                        Dg = [kb.sb(f"Dg{i}", [128, 128], BF16, sg) for i in range(6)]
                        EB = 8
                        NBATCH = 128 // EB
                        for t in range(NT):
                            ix = idx[t % 2]; gt_ = gate[t % 2]; at = act[t % 2]; gat = ga[t % 2]; xb = xbf[t % 2]
                            nxt = p2_ops(t + 1) if t + 1 < NT else []
                            per = -(-len(nxt) // NBATCH) if nxt else 0
                            op("act", lambda e: e.activation(out=xb.t[:], in_=h.t[:, t, :], func=AF.Copy), reads=[ht[t]], writes=[xb])
                            slots = {}
                            for b in range(NBATCH + 1):
                                if b >= 1:
                                    sl = slice((b - 1) * EB, b * EB)
                                    op("act", lambda e: e.activation(out=gat.t[:, sl], in_=at.t[:, sl], func=AF.Gelu), reads=[at], writes=[gat])
                                    op("dve", lambda e: e.tensor_tensor(out=gat.t[:, sl], in0=gat.t[:, sl], in1=gt_.t[:].rearrange("p h k -> p (h k)")[:, sl], op=ALU.mult),
                                       reads=[gat, gt_], writes=[gat])
                                if b >= 1:
                                    for ei in range((b - 1) * EB, b * EB):
                                        gbuf = gb_[slots[ei]]
                                        dg = Dg[dcount % len(Dg)]; dcount += 1
                                        op("dve", lambda e: e.tensor_scalar(out=dg.t[:], in0=identb.t[:], scalar1=gat.t[:, ei:ei + 1], scalar2=None, op0=ALU.mult),
                                           reads=[identb], sreads=[gat], writes=[dg])
                                        for half in range(2):
                                            yb = pb[2 * (t % 2) + half]
                                            op("pe", lambda e: e.matmul(out=yb.t[:, :], lhsT=dg.t[:], rhs=gbuf.t[:, D + half * 512:D + (half + 1) * 512],
                                                                        start=(ei == 0), stop=(ei == 127)), reads=[dg, gbuf], writes=[yb])
                                if b < NBATCH:
                                    for ei in range(b * EB, (b + 1) * EB):
                                        s_ = gcount % NB_G; gcount += 1; slots[ei] = s_
                                        gbuf = gb_[s_]
                                        kb.dma("pool", f"g{s_}", gbuf.t[:, :], uv_h[:, :], reads=[ix, scr[l]], writes=[gbuf], indirect=ix.t[:, ei:ei + 1])
                                        pr_ = prod[pcount % 2]; pcount += 1
                                        op("dve", lambda e: e.tensor_tensor(out=pr_.t[:], in0=gbuf.t[:, 0:D], in1=xb.t[:], op=ALU.mult), reads=[gbuf, xb], writes=[pr_])
                                        op("act", lambda e: e.activation(out=junkb.t[:], in_=pr_.t[:], func=AF.Copy, accum_out=at.t[:, ei:ei + 1]),
                                           reads=[pr_], writes=[junkb, at])
                                for _ in range(per):
                                    if nxt:
                                        nxt.pop(0)()
                            while nxt:
                                nxt.pop(0)()
                            if t == 0:
                                dump("act", at.t[:], [128, 128], F32, [at])
                            for half in range(2):
                                yb = pb[2 * (t % 2) + half]
                                op("dve", lambda e: e.scalar_tensor_tensor(out=y.t[:, half * 512:(half + 1) * 512], in0=h.t[:, t, half * 512:(half + 1) * 512], scalar=ALPHA,
                                                                           in1=yb.t[:, :], op0=ALU.mult, op1=ALU.add), reads=[ht[t], yb], writes=[y])
                            layer_norm_tile(t, y.t[:], g2, b2, y, st, mv, r2)
                        kb.barrier()
        ov = out_h.ap().rearrange("(n p) d -> p n d", p=128)
        for t in range(NT):
            kb.dma("sp", "out", ov[:, t, :], h.t[:, t, :], reads=[ht[t]])
        kb.barrier()
    return nc


def core_inputs(S, L, x_b, P, consts):
    m = {"x": np.ascontiguousarray(x_b.reshape(S, D))}
    m.update(P)
    m.update({"c_" + k: v for k, v in consts.items()})
    return m


def pack_params(inputs, L):
    f = lambda a: np.ascontiguousarray(np.asarray(a, dtype=np.float32))
    P = {
        "w_in": f(inputs["w_in"][:L]),
        "lam4": f(np.stack([inputs["lam_q1"][:L], inputs["lam_k1"][:L], inputs["lam_q2"][:L], inputs["lam_k2"][:L]], axis=1)),
        "subln_g": f(inputs["subln_g"][:L]),
        "w_o": f(inputs["w_o"][:L]),
        "ln_gb": f(np.stack([inputs["ln1_g"][:L], inputs["ln1_b"][:L], inputs["ln2_g"][:L], inputs["ln2_b"][:L]], axis=1)),
        "w_query": f(inputs["w_query"][:L]),
        "sub_keys": f(np.asarray(inputs["sub_keys"][:L]).reshape(L, 16, 128, 128)),
        "expert_u": f(np.asarray(inputs["expert_u"][:L]).reshape(L * NEXP, D)),
        "expert_v": f(np.asarray(inputs["expert_v"][:L]).reshape(L * NEXP, D)),
    }
    return P


def kernel(**inputs):
    x = np.asarray(inputs["x"], dtype=np.float32)
    B, S, _ = x.shape
    L = DEPTH
    P = pack_params(inputs, L)
    consts = host_consts(S)
    nc = build_program(S, L)
    in_maps = [core_inputs(S, L, x[b], P, consts) for b in range(B)]
    res = run_bass_kernel_spmd(nc, in_maps, core_ids=list(range(B)))
    out = np.stack([np.asarray(r["out"]).reshape(S, D) for r in res.results], axis=0)
    return out.astype(np.float32)
```

```python
import math
import numpy as np
from contextlib import ExitStack
import ml_dtypes
import concourse.bass as bass
import concourse.mybir as mybir
from concourse.bass_utils import run_bass_kernel_spmd

F32 = mybir.dt.float32; BF16 = mybir.dt.bfloat16; I32 = mybir.dt.int32; U32 = mybir.dt.uint32; U16 = mybir.dt.uint16
AF = mybir.ActivationFunctionType; ALU = mybir.AluOpType; AX = mybir.AxisListType

D = 1024; DEPTH = 4; NEXP = 16384
ALPHA = (2.0 * DEPTH) ** 0.25
LN_EPS = 1e-5; RMS_EPS = 1e-5
SCALE = 0.125
NB_G = 16


class Buf:
    def __init__(self, name, t=None):
        self.name = name; self.t = t; self.w = None; self.r = {}

    def __getitem__(self, k):
        return self.t[k]


class DSem:
    def __init__(self, sem):
        self.sem = sem; self.count = 0


class Eng:
    def __init__(self, name, eng, sem):
        self.name = name; self.eng = eng; self.sem = sem; self.count = 0; self.seen = {}


class KB:
    def __init__(self, nc, es):
        self.nc = nc; self.es = es
        self.E = {}
        for n, e in (("pe", nc.tensor), ("act", nc.scalar), ("dve", nc.vector), ("pool", nc.gpsimd), ("sp", nc.sync)):
            self.E[n] = Eng(n, e, es.enter_context(nc.semaphore("sem_" + n)))
        self.dsems = {}
        self.semobj = {}
        for E in self.E.values():
            self.semobj[id(E.sem)] = (E.sem, E)
        self.uid = 0

    def dsem(self, name):
        if name not in self.dsems:
            d = DSem(self.es.enter_context(self.nc.semaphore("ds_" + name)))
            self.dsems[name] = d; self.semobj[id(d.sem)] = (d.sem, d)
        return self.dsems[name]

    def sb(self, name, shape, dtype, es=None):
        self.uid += 1
        t = (es or self.es).enter_context(self.nc.sbuf_tensor(f"{name}_{self.uid}", list(shape), dtype))
        return Buf(name, t)

    def ps(self, name, shape, dtype, es=None):
        self.uid += 1
        t = (es or self.es).enter_context(self.nc.psum_tensor(f"{name}_{self.uid}", list(shape), dtype))
        return Buf(name, t)

    def _wait(self, E, toks, strict=()):
        best = {}
        for (sid, v) in list(toks) + list(strict):
            if best.get(sid, 0) < v:
                best[sid] = v
        sown = 0
        for (sid, v) in strict:
            if self.semobj[sid][1] is E:
                sown = max(sown, v)
        for sid, v in best.items():
            sem, owner = self.semobj[sid]
            if isinstance(owner, DSem):
                v = owner.count
            if owner is E and E.name == "pe":
                continue
            if E.seen.get(sid, 0) < v:
                E.eng.wait_ge(sem, v); E.seen[sid] = v

    def _deps(self, reads, writes):
        toks = []
        for b in reads:
            if b.w:
                toks.append(b.w)
        for b in writes:
            if b.w:
                toks.append(b.w)
            toks.extend(b.r.items())
        return toks

    def _mark(self, tok, reads, writes):
        sid, v = tok
        for b in reads:
            if b.r.get(sid, 0) < v:
                b.r[sid] = v
        for b in writes:
            b.w = tok; b.r = {}

    def op(self, en, fn, reads=(), writes=(), sreads=()):
        E = self.E[en]
        self._wait(E, self._deps(reads, writes), [b.w for b in sreads if b.w])
        ins = fn(E.eng)
        E.count += 1
        ins.then_inc(E.sem, 1)
        self._mark((id(E.sem), E.count), list(reads) + list(sreads), writes)
        return ins

    def dma(self, qn, dname, out, in_, reads=(), writes=(), indirect=None):
        E = self.E[qn]; d = self.dsem(dname)
        self._wait(E, self._deps(reads, writes))
        if indirect is not None:
            ins = E.eng.indirect_dma_start(out=out, out_offset=None, in_=in_,
                                           in_offset=bass.IndirectOffsetOnAxis(ap=indirect, axis=0))
        else:
            ins = E.eng.dma_start(out=out, in_=in_)
        d.count += 16
        ins.then_inc(d.sem, 16)
        self._mark((id(d.sem), d.count), reads, writes)
        return ins

    def barrier(self):
        toks = [(id(E.sem), E.count) for E in self.E.values() if E.count] + \
               [(id(d.sem), d.count) for d in self.dsems.values() if d.count]
        for E in self.E.values():
            self._wait(E, toks)


def bc(ap, shape):
    return ap.to_broadcast(list(shape))


def host_consts(S):
    c = {}
    c["ident"] = np.eye(128, dtype=np.float32)
    j = np.arange(128)
    c["tri"] = (j[:, None] >= j[None, :]).astype(np.float32)
    c["ones"] = np.ones((128, 128), np.float32)
    q = np.arange(512)
    mc = np.zeros((128, 4, 512), np.float32); ms = np.zeros((128, 4, 512), np.float32)
    for jj in range(4):
        mc[:, jj, :] = (q[None, :] >= jj * 128 + j[:, None])
        ms[:, jj, :] = (q[None, :] > jj * 128 + j[:, None])
    c["maskc"] = mc.astype(ml_dtypes.bfloat16); c["masks"] = (-800.0 * (1.0 - ms)).astype(ml_dtypes.bfloat16)
    pos = np.arange(S)
    hi = (pos // 128) * 128.0; lo = (pos % 128) * 1.0
    posk = np.zeros((4, S), np.float32); posk[0] = hi; posk[1] = lo; posk[2] = 1; posk[3] = 1
    c["posk"] = posk.astype(ml_dtypes.bfloat16)
    posq = np.zeros((4, 4, S), np.float32)
    for hh in range(4):
        cc = 2.0 ** (-8.0 * (hh + 1) / 4) / SCALE
        posq[0, hh] = cc; posq[1, hh] = cc; posq[2, hh] = -cc * hi; posq[3, hh] = -cc * lo
    c["posq"] = posq.astype(ml_dtypes.bfloat16)
    c["iota16"] = np.tile(np.arange(16, dtype=np.float32)[None, :], (128, 1))
    c["zeros"] = np.zeros((128, 512), ml_dtypes.bfloat16)
    return c


def build_program(S, L, do_attn=True, do_peer=True, dbg=False, stop=None):
    NT = S // 128
    NQB = S // 512 if S >= 512 else 1
    QBW = min(S, 512)
    TPB = QBW // 128
    nc = bass.Bass("TRN2", target_bir_lowering=False)
    dt = nc.dram_tensor
    x_h = dt("x", [S, D], F32, kind="ExternalInput")
    w_in_h = dt("w_in", [L, D, 3072], F32, kind="ExternalInput")
    lam_h = dt("lam4", [L, 4, 64], F32, kind="ExternalInput")
    subg_h = dt("subln_g", [L, 128], F32, kind="ExternalInput")
    w_o_h = dt("w_o", [L, D, D], F32, kind="ExternalInput")
    lng_h = dt("ln_gb", [L, 4, D], F32, kind="ExternalInput")
    wq_h = dt("w_query", [L, D, 2048], F32, kind="ExternalInput")
    keys_h = dt("sub_keys", [L, 16, 128, 128], F32, kind="ExternalInput")
    eu_h = dt("expert_u", [L * NEXP, D], F32, kind="ExternalInput")
    ev_h = dt("expert_v", [L * NEXP, D], F32, kind="ExternalInput")
    c_ident = dt("c_ident", [128, 128], F32, kind="ExternalInput")
    c_tri = dt("c_tri", [128, 128], F32, kind="ExternalInput")
    c_ones = dt("c_ones", [128, 128], F32, kind="ExternalInput")
    c_maskc = dt("c_maskc", [128, 4, 512], BF16, kind="ExternalInput")
    c_masks = dt("c_masks", [128, 4, 512], BF16, kind="ExternalInput")
    c_posk = dt("c_posk", [4, S], BF16, kind="ExternalInput")
    c_posq = dt("c_posq", [4, 4, S], BF16, kind="ExternalInput")
    c_iota = dt("c_iota16", [128, 16], F32, kind="ExternalInput")
    c_zeros = dt("c_zeros", [128, 512], BF16, kind="ExternalInput")
    out_h = dt("out", [S, D], F32, kind="ExternalOutput")
    uv_h = dt("uv_scr", [L * NEXP, 2048], BF16, kind="Internal")
    dbg_h = {}
    if dbg:
        dbg_h["h1"] = dt("dbg_h1", [S, D], F32, kind="ExternalOutput")

    with ExitStack() as es:
        kb = KB(nc, es)
        op = kb.op

        def dump(name, ap, shape, dtype, reads):
            if not dbg or name in dbg_h:
                return
            dbg_h[name] = dt("dbg_" + name, list(shape), dtype, kind="ExternalOutput")
            kb.dma("sp", "dbg", dbg_h[name].ap(), ap, reads=reads)
        h = kb.sb("h", [128, NT, D], F32)
        ident = kb.sb("ident", [128, 128], F32)
        tri = kb.sb("tri", [128, 128], F32)
        ones = kb.sb("ones", [128, 128], F32)
        onesb = kb.sb("onesb", [128, 128], BF16)
        zeros = kb.sb("zeros", [128, 512], BF16)
        iota16 = kb.sb("iota16", [128, 16], F32)
        pb = [kb.ps(f"pb{i}", [128, 512], F32) for i in range(8)]
        ht = [Buf(f"h{t}") for t in range(NT)]

        kb.dma("sp", "c0", ident.t[:], c_ident.ap(), writes=[ident])
        kb.dma("sp", "c0", tri.t[:], c_tri.ap(), writes=[tri])
        kb.dma("sp", "c0", ones.t[:], c_ones.ap(), writes=[ones])
        kb.dma("sp", "c0", zeros.t[:], c_zeros.ap(), writes=[zeros])
        kb.dma("sp", "c0", iota16.t[:], c_iota.ap(), writes=[iota16])
        xv = x_h.ap().rearrange("(n p) d -> p n d", p=128)
        for t in range(NT):
            kb.dma("sp", "xin", h.t[:, t, :], xv[:, t, :], writes=[ht[t]])
        op("pool", lambda e: e.tensor_copy(out=onesb.t[:], in_=ones.t[:]), reads=[ones], writes=[onesb])
        identb = kb.sb("identb", [128, 128], BF16)
        op("pool", lambda e: e.tensor_copy(out=identb.t[:], in_=ident.t[:]), reads=[ident], writes=[identb])

        def build_hT(hT):
            for t in range(NT):
                for half in range(2):
                    bank = pb[(2 * t + half) % 4]
                    for c4 in range(4):
                        c = half * 4 + c4
                        op("pe", lambda e: e.transpose(out=bank.t[:, c4 * 128:(c4 + 1) * 128], in_=h.t[:, t, c * 128:(c + 1) * 128],
                                                       identity=ident.t[:]), reads=[ht[t], ident], writes=[bank])
                    op("act", lambda e: e.activation(out=hT.t[:, half * 4:half * 4 + 4, t * 128:(t + 1) * 128],
                                                     in_=bank.t[:].rearrange("p (c n) -> p c n", c=4), func=AF.Copy),
                       reads=[bank], writes=[hT])

        def layer_norm_tile(t, src, gt, bt_, tmp, st, mv, r2):
            for c in range(2):
                op("dve", lambda e: e.bn_stats(out=st.t[:, c, :], in_=src[:, c * 512:(c + 1) * 512]), reads=[tmp, ht[t]], writes=[st])
            op("dve", lambda e: e.bn_aggr(out=mv.t[:], in_=st.t[:].rearrange("p a b -> p (a b)")), reads=[st], writes=[mv])
            op("act", lambda e: e.activation(out=r2.t[:, 0:1], in_=mv.t[:, 1:2], func=AF.Ln, bias=epsb.t[:, 0:1], scale=1.0), reads=[mv, epsb], writes=[r2])
            op("act", lambda e: e.activation(out=r2.t[:, 1:2], in_=r2.t[:, 0:1], func=AF.Exp, scale=-0.5), reads=[r2], writes=[r2])
            op("dve", lambda e: e.tensor_scalar(out=tmp.t[:], in0=src, scalar1=mv.t[:, 0:1], scalar2=r2.t[:, 1:2],
                                                op0=ALU.subtract, op1=ALU.mult), reads=[tmp, ht[t]], sreads=[mv, r2], writes=[tmp])
            op("dve", lambda e: e.tensor_tensor(out=tmp.t[:], in0=tmp.t[:], in1=gt.t[:], op=ALU.mult), reads=[tmp, gt], writes=[tmp])
            op("dve", lambda e: e.tensor_tensor(out=h.t[:, t, :], in0=tmp.t[:], in1=bt_.t[:], op=ALU.add), reads=[tmp, bt_], writes=[ht[t]])

        scr = [Buf(f"scr{i}") for i in range(L)]

        def convert_layer(cl):
            CH = 2048
            d = kb.dsem(f"cv{cl}")
            for c in range(NEXP // CH):
                r0 = cl * NEXP + c * CH
                if d.count >= 16 * 4:
                    kb.E["pool"].eng.wait_ge(d.sem, d.count - 16 * 3)
                kb.dma("pool", f"cv{cl}", uv_h[r0:r0 + CH, 0:D], eu_h[r0:r0 + CH, :], writes=[scr[cl]])
                kb.dma("pool", f"cv{cl}", uv_h[r0:r0 + CH, D:2 * D], ev_h[r0:r0 + CH, :], writes=[scr[cl]])

        epsb = kb.sb("epsb", [128, 1], F32)
        op("pool", lambda e: e.memset(epsb.t[:], LN_EPS), writes=[epsb])

        for l in range(L):
            li = 0.8 - 0.6 * math.exp(-0.3 * l)
            if do_attn:
                with ExitStack() as sa:
                    hT = kb.sb("hT", [128, 8, S], BF16, sa)
                    build_hT(hT)
                    wst = kb.sb("wst", [128, 8, 384], F32, sa)
                    wbf = kb.sb("wbf", [128, 8, 384], BF16, sa)
                    wost = kb.sb("wost", [128, D], F32, sa)
                    wobf = kb.sb("wobf", [128, D], BF16, sa)
                    QT = kb.sb("QT", [128, S], BF16, sa)
                    KT = kb.sb("KT", [128, S], BF16, sa)
                    V = kb.sb("V", [128, NT, 128], BF16, sa)
                    mixT = kb.sb("mixT", [128, S], BF16, sa)
                    maskc = kb.sb("maskc", [128, 4, 512], BF16, sa)
                    masks = kb.sb("masks", [128, 4, 512], BF16, sa)
                    posk = kb.sb("posk", [4, S], BF16, sa)
                    posq = kb.sb("posq", [4, S], BF16, sa)
                    NEB = 6; NEZ = 5; NSP = 4; NE2 = 3
                    Eb = [kb.sb(f"E{i}", [128, 512], BF16, sa) for i in range(NEB)]
                    ez = [kb.sb(f"ez{i}", [128, 512], F32, sa) for i in range(NEZ)]
                    spb = [kb.sb(f"sp{i}", [128, 512], BF16, sa) for i in range(NSP)]
                    Rbs = [kb.sb(f"Rb{i}", [128, 512], BF16, sa) for i in range(2)]
                    trib = kb.sb("trib", [128, 128], BF16, sa)
                    ZbA = Buf("ZbA"); ZbB = Buf("ZbB")
                    e2 = [kb.sb(f"e2{i}", [128, 512], F32, sa) for i in range(NE2)]
                    R = kb.sb("R", [128, 512], F32, sa)
                    lamt = kb.sb("lamt", [128, 4, 64], F32, sa)
                    lamp = kb.sb("lamp", [128, 2, 64], F32, sa)
                    lams = kb.sb("lams", [128, 4], F32, sa)
                    gsc = kb.sb("gsc", [128, 128], F32, sa)
                    rz = kb.sb("rz", [128, 8], F32, sa)
                    ss = kb.sb("ss", [128, 4], F32, sa)
                    rstd = kb.sb("rstd", [128, 4], F32, sa)
                    tq = [kb.sb(f"tq{i}", [128, 128], F32, sa) for i in range(2)]
                    dq_ = [kb.sb(f"dq{i}", [128, 128], F32, sa) for i in range(4)]
                    dn = [kb.sb(f"dn{i}", [128, 128], F32, sa) for i in range(2)]
                    junk = kb.sb("junk", [128, 128], F32, sa)
                    osb = kb.sb("osb", [128, 512], F32, sa)
                    g1 = kb.sb("g1", [128, D], F32, sa)
                    b1 = kb.sb("b1", [128, D], F32, sa)
                    lntmp = kb.sb("lntmp", [128, D], F32, sa)
                    st = kb.sb("st", [128, 2, 6], F32, sa)
                    mv = kb.sb("mv", [128, 2], F32, sa)
                    r2 = kb.sb("r2", [128, 2], F32, sa)
                    rmseps = kb.sb("rmseps", [128, 1], F32, sa)

                    kb.dma("sp", "c1", maskc.t[:], c_maskc.ap(), writes=[maskc])
                    kb.dma("sp", "c1", masks.t[:], c_masks.ap(), writes=[masks])
                    kb.dma("sp", "c1", posk.t[:], c_posk.ap(), writes=[posk])
                    kb.dma("sp", "c1", lamt.t[:].rearrange("p a b -> p (a b)"),
                           bc(lam_h[l:l + 1, :, :].rearrange("o a b -> o (a b)"), [128, 256]), writes=[lamt])
                    kb.dma("sp", "c1", gsc.t[:], bc(subg_h[l:l + 1, :], [128, 128]), writes=[gsc])
                    kb.dma("sp", "c1", g1.t[:], bc(lng_h[l, 0:1, :], [128, D]), writes=[g1])
                    kb.dma("sp", "c1", b1.t[:], bc(lng_h[l, 1:2, :], [128, D]), writes=[b1])
                    op("pool", lambda e: e.memset(rmseps.t[:], RMS_EPS), writes=[rmseps])
                    op("pool", lambda e: e.tensor_copy(out=trib.t[:], in_=tri.t[:]), reads=[tri], writes=[trib])
                    op("dve", lambda e: e.tensor_tensor(out=lamp.t[:, 0, :], in0=lamt.t[:, 0, :], in1=lamt.t[:, 1, :], op=ALU.mult), reads=[lamt], writes=[lamp])
                    op("dve", lambda e: e.tensor_tensor(out=lamp.t[:, 1, :], in0=lamt.t[:, 2, :], in1=lamt.t[:, 3, :], op=ALU.mult), reads=[lamt], writes=[lamp])
                    op("dve", lambda e: e.tensor_reduce(out=lams.t[:, 0:2], in_=lamp.t[:], axis=AX.X, op=ALU.add), reads=[lamp], writes=[lams])
                    op("act", lambda e: e.activation(out=lams.t[:, 0:2], in_=lams.t[:, 0:2], func=AF.Exp), reads=[lams], writes=[lams])
                    op("dve", lambda e: e.tensor_tensor(out=lams.t[:, 2:3], in0=lams.t[:, 1:2], in1=lams.t[:, 0:1], op=ALU.subtract), reads=[lams], writes=[lams])
                    op("dve", lambda e: e.tensor_scalar(out=lams.t[:, 3:4], in0=lams.t[:, 2:3], scalar1=-li, scalar2=None, op0=ALU.add), reads=[lams], writes=[lams])
                    mlam = lams.t[:, 3:4]
                    op("act", lambda e: e.activation(out=gsc.t[:], in_=gsc.t[:], func=AF.Copy, scale=(1.0 - li)), reads=[gsc], writes=[gsc])

                    for g in range(8):
                        is_diff = g < 4
                        if is_diff:
                            cq, ck, cv = g * 128, 512 + g * 128, 1024 + g * 128
                        else:
                            cq, ck, cv = 1536 + (g - 4) * 128, 2048 + (g - 4) * 128, 2560 + (g - 4) * 128
                        wv = w_in_h[l].rearrange("(c p) n -> p c n", p=128)
                        for j, c0 in enumerate((cq, ck, cv)):
                            kb.dma("sp", "wst", wst.t[:, :, j * 128:(j + 1) * 128], wv[:, :, c0:c0 + 128], writes=[wst])
                        kb.dma("sp", "wost", wost.t[:], w_o_h[l, g * 128:(g + 1) * 128, :], writes=[wost])
                        op("dve", lambda e: e.tensor_copy(out=wbf.t[:], in_=wst.t[:]), reads=[wst], writes=[wbf])
                        op("dve", lambda e: e.tensor_copy(out=wobf.t[:], in_=wost.t[:]), reads=[wost], writes=[wobf])
                        if g == 0:
                            convert_layer(l)
                        if is_diff:
                            kb.dma("sp", "posq", posq.t[:, :], c_posq[:, g, :], writes=[posq])
                        for j, dst in ((0, QT), (1, KT)):
                            for blk in range(NQB):
                                bank = pb[(2 * j + blk) % 4]
                                for c in range(8):
                                    op("pe", lambda e: e.matmul(out=bank.t[:, 0:QBW], lhsT=wbf.t[:, c, j * 128:(j + 1) * 128],
                                                                rhs=hT.t[:, c, blk * QBW:(blk + 1) * QBW], start=(c == 0), stop=(c == 7)),
                                       reads=[wbf, hT], writes=[bank])
                                op("act", lambda e: e.activation(out=dst.t[:, blk * QBW:(blk + 1) * QBW], in_=bank.t[:, 0:QBW], func=AF.Copy),
                                   reads=[bank], writes=[dst])
                        for t4 in range(0, NT, 4):
                            bank = pb[(t4 // 4) % 4]
                            nt4 = min(4, NT - t4)
                            for tt in range(nt4):
                                t = t4 + tt
                                for c in range(8):
                                    op("pe", lambda e: e.matmul(out=bank.t[:, tt * 128:(tt + 1) * 128], lhsT=hT.t[:, c, t * 128:(t + 1) * 128],
                                                                rhs=wbf.t[:, c, 256:384], start=(c == 0), stop=(c == 7)),
                                       reads=[wbf, hT], writes=[bank])
                            op("act", lambda e: e.activation(out=V.t[:, t4:t4 + nt4, :], in_=bank.t[:, 0:nt4 * 128].rearrange("p (a b) -> p a b", b=128),
                                                             func=AF.Copy), reads=[bank], writes=[V])
                        steps = []
                        if is_diff:
                            for b in range(NQB):
                                q0 = b * QBW
                                kts = list(range(b * TPB + TPB))
                                if b % 2 == 0:
                                    O = [pb[4], pb[5]]; Zb = ZbA; zc = 0
                                else:
                                    O = [pb[2], pb[3]]; Zb = ZbB; zc = 8
                                for m in range(2):
                                    for ki, kt in enumerate(kts):
                                        first = (m == 0 and ki == 0); last = (m == 1 and ki == len(kts) - 1)

                                        def s1(b=b, q0=q0, m=m, kt=kt, sidx=len(steps)):
                                            pr = slice(m * 64, (m + 1) * 64)
                                            j = kt - b * TPB
                                            sbank = pb[sidx % 2]; E = Eb[sidx % NEB]
                                            op("pe", lambda e: e.matmul(out=sbank.t[:, 0:QBW], lhsT=KT.t[pr, kt * 128:(kt + 1) * 128],
                                                                        rhs=QT.t[pr, q0:q0 + QBW], start=True, stop=False),
                                               reads=[KT, QT], writes=[sbank])
                                            op("pe", lambda e: e.matmul(out=sbank.t[:, 0:QBW], lhsT=posk.t[0:4, kt * 128:(kt + 1) * 128],
                                                                        rhs=posq.t[0:4, q0:q0 + QBW], start=False, stop=True),
                                               reads=[posk, posq], writes=[sbank])
                                            op("act", lambda e: e.activation(out=E.t[:, 0:QBW], in_=sbank.t[:, 0:QBW], func=AF.Exp, scale=SCALE),
                                               reads=[sbank], writes=[E])
                                            if j >= 0:
                                                op("dve", lambda e: e.tensor_tensor(out=E.t[:, 0:QBW], in0=E.t[:, 0:QBW], in1=maskc.t[:, j, 0:QBW], op=ALU.mult),
                                                   reads=[E, maskc], writes=[E])

                                        def s2(b=b, q0=q0, m=m, kt=kt, sidx=len(steps), first=first, last=last, O=O, Zb=Zb, zc=zc, kts=kts):
                                            j = kt - b * TPB
                                            E = Eb[sidx % NEB]
                                            if first:
                                                for mm in range(2):
                                                    op("pe", lambda e: e.matmul(out=O[mm].t[:, :], lhsT=zeros.t[:, 0:128], rhs=zeros.t[:, :], start=True, stop=False),
                                                       reads=[zeros], writes=[O[mm]])
                                                op("pe", lambda e: e.matmul(out=pb[6].t[:, zc:zc + 8], lhsT=zeros.t[:, 0:128], rhs=zeros.t[:, 0:8], start=True, stop=False),
                                                   reads=[zeros], writes=[Zb])
                                            for tt in range(max(j, 0), TPB):
                                                op("pe", lambda e: e.matmul(out=O[m].t[:, tt * 128:(tt + 1) * 128], lhsT=E.t[:, tt * 128:(tt + 1) * 128],
                                                                            rhs=V.t[:, kt, :], start=False, stop=(kt == kts[-1]), skip_group_check=True),
                                                   reads=[E, V], writes=[O[m]])
                                                op("pe", lambda e: e.matmul(out=pb[6].t[:, zc + m * 4 + tt:zc + m * 4 + tt + 1], lhsT=E.t[:, tt * 128:(tt + 1) * 128],
                                                                            rhs=onesb.t[:, 0:1], start=False, stop=(kt == kts[-1]), skip_group_check=True),
                                                   reads=[E, onesb], writes=[Zb])
                                            if not last:
                                                return
                                            op("dve", lambda e: e.reciprocal(out=rz.t[:], in_=pb[6].t[:, zc:zc + 8]), reads=[Zb], writes=[rz])
                                            op("dve", lambda e: e.tensor_scalar(out=rz.t[:, 4:8], in0=rz.t[:, 4:8], scalar1=mlam, scalar2=None, op0=ALU.mult),
                                               reads=[rz], sreads=[lams], writes=[rz])
                                            for tt in range(TPB):
                                                tqb = tq[tt % 2]; dd = dq_[tt]
                                                op("act", lambda e: e.activation(out=tqb.t[:], in_=O[0].t[:, tt * 128:(tt + 1) * 128], func=AF.Copy, scale=rz.t[:, tt:tt + 1]),
                                                   reads=[O[0]], sreads=[rz], writes=[tqb])
                                                op("dve", lambda e: e.scalar_tensor_tensor(out=dd.t[:], in0=O[1].t[:, tt * 128:(tt + 1) * 128], scalar=rz.t[:, 4 + tt:5 + tt],
                                                                                           in1=tqb.t[:], op0=ALU.mult, op1=ALU.add), reads=[O[1], tqb], sreads=[rz], writes=[dd])
                                                op("act", lambda e: e.activation(out=junk.t[:], in_=dd.t[:], func=AF.Square, accum_out=ss.t[:, tt:tt + 1]),
                                                   reads=[dd], writes=[junk, ss])
                                            op("act", lambda e: e.activation(out=rstd.t[:, 0:TPB], in_=ss.t[:, 0:TPB], func=AF.Ln, scale=1.0 / 128, bias=rmseps.t[:, 0:1]),
                                               reads=[ss, rmseps], writes=[rstd])
                                            op("act", lambda e: e.activation(out=rstd.t[:, 0:TPB], in_=rstd.t[:, 0:TPB], func=AF.Exp, scale=-0.5), reads=[rstd], writes=[rstd])
                                            tb = pb[7]
                                            for tt in range(TPB):
                                                dnb = dn[tt % 2]
                                                op("dve", lambda e: e.scalar_tensor_tensor(out=dnb.t[:], in0=dq_[tt].t[:], scalar=rstd.t[:, tt:tt + 1], in1=gsc.t[:],
                                                                                           op0=ALU.mult, op1=ALU.mult), reads=[dq_[tt], gsc], sreads=[rstd], writes=[dnb])
                                                op("pe", lambda e: e.transpose(out=tb.t[:, tt * 128:(tt + 1) * 128], in_=dnb.t[:], identity=ident.t[:]),
                                                   reads=[dnb, ident], writes=[tb])
                                            op("act", lambda e: e.activation(out=mixT.t[:, q0:q0 + QBW], in_=tb.t[:, 0:QBW], func=AF.Copy), reads=[tb], writes=[mixT])
                                        steps.append((s1, s2))
                            SKEW = (0, 3)
                        else:
                            for b in range(NQB):
                                q0 = b * QBW
                                kts = list(range(b * TPB + TPB))
                                O = pb[4 + b % 2]
                                for p in range(2):
                                    for si, kt in enumerate(reversed(kts)):
                                        first = (p == 0 and si == 0); last = (p == 1 and kt == 0)

                                        def s1(b=b, q0=q0, p=p, kt=kt, sidx=len(steps)):
                                            pr = slice(p * 64, (p + 1) * 64)
                                            j = kt - b * TPB
                                            zbank = pb[sidx % 2]; ezb = ez[sidx % NEZ]; sp_ = spb[sidx % NSP]
                                            op("pe", lambda e: e.matmul(out=zbank.t[:, 0:QBW], lhsT=KT.t[pr, kt * 128:(kt + 1) * 128],
                                                                        rhs=QT.t[pr, q0:q0 + QBW], start=True, stop=(j < 0)), reads=[KT, QT], writes=[zbank])
                                            if j >= 0:
                                                op("pe", lambda e: e.matmul(out=zbank.t[:, 0:QBW], lhsT=identb.t[:], rhs=masks.t[:, j, 0:QBW], start=False, stop=True),
                                                   reads=[identb, masks], writes=[zbank])
                                            op("act", lambda e: e.activation(out=ezb.t[:, 0:QBW], in_=zbank.t[:, 0:QBW], func=AF.Exp, scale=SCALE),
                                               reads=[zbank], writes=[ezb])
                                            op("act", lambda e: e.activation(out=sp_.t[:, 0:QBW], in_=ezb.t[:, 0:QBW], func=AF.Ln, bias=ones.t[:, 0:1], scale=1.0),
                                               reads=[ezb, ones], writes=[sp_])

                                        def s2(b=b, kt=kt, si=si, sidx=len(steps)):
                                            cbank = pb[2 + sidx % 2]; sp_ = spb[sidx % NSP]; e2b = e2[sidx % NE2]
                                            Rb = Rbs[(sidx + 1) % 2]
                                            Rbn = Rbs[sidx % 2]
                                            op("pe", lambda e: e.matmul(out=cbank.t[:, 0:QBW], lhsT=trib.t[:], rhs=sp_.t[:, 0:QBW], start=True, stop=(si == 0)),
                                               reads=[trib, sp_], writes=[cbank])
                                            if si > 0:
                                                op("pe", lambda e: e.matmul(out=cbank.t[:, 0:QBW], lhsT=onesb.t[:], rhs=Rb.t[:, 0:QBW], start=False, stop=True),
                                                   reads=[onesb, Rb], writes=[cbank])
                                            if si == 0:
                                                op("dve", lambda e: e.tensor_copy(out=R.t[:, 0:QBW], in_=sp_.t[:, 0:QBW]), reads=[sp_], writes=[R])
                                                if kt > 0:
                                                    op("dve", lambda e: e.tensor_copy(out=Rbn.t[:, 0:QBW], in_=sp_.t[:, 0:QBW]), reads=[sp_], writes=[Rbn])
                                            elif kt > 0:
                                                op("dve", lambda e: e.tensor_tensor(out=R.t[:, 0:QBW], in0=R.t[:, 0:QBW], in1=sp_.t[:, 0:QBW], op=ALU.add),
                                                   reads=[R, sp_], writes=[R])
                                                op("dve", lambda e: e.tensor_copy(out=Rbn.t[:, 0:QBW], in_=R.t[:, 0:QBW]), reads=[R], writes=[Rbn])
                                            op("act", lambda e: e.activation(out=e2b.t[:, 0:QBW], in_=cbank.t[:, 0:QBW], func=AF.Exp, scale=-1.0),
                                               reads=[cbank], writes=[e2b])

                                        def s3(b=b, q0=q0, p=p, kt=kt, sidx=len(steps), first=first, last=last, O=O):
                                            j = kt - b * TPB
                                            ezb = ez[sidx % NEZ]; e2b = e2[sidx % NE2]; W = Eb[sidx % NEB]
                                            op("dve", lambda e: e.tensor_tensor(out=W.t[:, 0:QBW], in0=ezb.t[:, 0:QBW], in1=e2b.t[:, 0:QBW], op=ALU.mult),
                                               reads=[ezb, e2b], writes=[W])
                                            if first:
                                                op("pe", lambda e: e.matmul(out=O.t[:, :], lhsT=zeros.t[:, 0:128], rhs=zeros.t[:, :], start=True, stop=False),
                                                   reads=[zeros], writes=[O])
                                            for tt in range(max(j, 0), TPB):
                                                op("pe", lambda e: e.matmul(out=O.t[:, tt * 128 + p * 64:tt * 128 + (p + 1) * 64], lhsT=W.t[:, tt * 128:(tt + 1) * 128],
                                                                            rhs=V.t[:, kt, p * 64:(p + 1) * 64], start=False, stop=(kt == 0), skip_group_check=True),
                                                   reads=[W, V], writes=[O])
                                            if not last:
                                                return
                                            op("act", lambda e: e.activation(out=osb.t[:, 0:QBW], in_=O.t[:, 0:QBW], func=AF.Copy), reads=[O], writes=[osb])
                                            tb = pb[7]
                                            for tt in range(TPB):
                                                op("pe", lambda e: e.transpose(out=tb.t[:, tt * 128:(tt + 1) * 128], in_=osb.t[:, tt * 128:(tt + 1) * 128], identity=ident.t[:]),
                                                   reads=[osb, ident], writes=[tb])
                                            op("act", lambda e: e.activation(out=mixT.t[:, q0:q0 + QBW], in_=tb.t[:, 0:QBW], func=AF.Copy), reads=[tb], writes=[mixT])
                                        steps.append((s1, s2, s3))
                            SKEW = (0, 2, 4)
                        for n in range(len(steps) + max(SKEW)):
                            for k in reversed(range(len(SKEW))):
                                i = n - SKEW[k]
                                if 0 <= i < len(steps):
                                    steps[i][k]()
                        for t in range(NT):
                            for half in range(2):
                                bank = pb[2 + (2 * t + half) % 2]
                                op("pe", lambda e: e.matmul(out=bank.t[:, :], lhsT=mixT.t[:, t * 128:(t + 1) * 128], rhs=wobf.t[:, half * 512:(half + 1) * 512],
                                                            start=True, stop=True), reads=[mixT, wobf], writes=[bank])
                                hs = h.t[:, t, half * 512:(half + 1) * 512]
                                if g == 0:
                                    op("dve", lambda e: e.scalar_tensor_tensor(out=hs, in0=hs, scalar=ALPHA, in1=bank.t[:, :], op0=ALU.mult, op1=ALU.add),
                                       reads=[bank, ht[t]], writes=[ht[t]])
                                else:
                                    op("dve", lambda e: e.tensor_tensor(out=hs, in0=hs, in1=bank.t[:, :], op=ALU.add), reads=[bank, ht[t]], writes=[ht[t]])
                    for t in range(NT):
                        layer_norm_tile(t, h.t[:, t, :], g1, b1, lntmp, st, mv, r2)
                    kb.barrier()
            if dbg and l == 0:
                hv = dbg_h["h1"].ap().rearrange("(n p) d -> p n d", p=128)
                for t in range(NT):
                    kb.dma("sp", "dbg", hv[:, t, :], h.t[:, t, :], reads=[ht[t]])
            if do_peer:
                if not do_attn:
                    convert_layer(l)
                with ExitStack() as sp1:
                    top_s = kb.sb("top_s", [128, NT, 16, 16], F32, sp1)
                    top_i = kb.sb("top_i", [128, NT, 16, 16], U16, sp1)
                    with ExitStack() as sq:
                        hT = kb.sb("hT", [128, 8, S], BF16, sq)
                        build_hT(hT)
                        kst = kb.sb("kst", [128, 16, 128], F32, sq)
                        keysT = kb.sb("keysT", [128, 16, 128], BF16, sq)
                        wqst = [kb.sb(f"wqst{i}", [128, 8, 128], F32, sq) for i in range(2)]
                        wqbf = [kb.sb(f"wqbf{i}", [128, 8, 128], BF16, sq) for i in range(2)]
                        qT = [kb.sb(f"qT{i}", [128, S], BF16, sq) for i in range(2)]
                        sc = [kb.sb(f"sc{i}", [128, S], F32, sq) for i in range(2)]
                        mrs = [kb.sb(f"mr{i}", [128, 128], F32, sq) for i in range(4)]
                        tsb = [Buf(f"tsb{i}") for i in range(NT)]; tib = [Buf(f"tib{i}") for i in range(NT)]
                        kb.dma("sp", "kst", kst.t[:], keys_h[l].rearrange("a n c -> n a c"), writes=[kst])
                        for hp in range(16):
                            bank = pb[hp % 4]
                            op("pe", lambda e: e.transpose(out=bank.t[:, 0:128], in_=kst.t[:, hp, :], identity=ident.t[:]), reads=[kst, ident], writes=[bank])
                            op("act", lambda e: e.activation(out=keysT.t[:, hp, :], in_=bank.t[:, 0:128], func=AF.Copy), reads=[bank], writes=[keysT])
                        wqv = wq_h[l].rearrange("(c p) n -> p c n", p=128)
                        for hp in range(16):
                            ws = wqst[hp % 2]; wb = wqbf[hp % 2]; qt_ = qT[hp % 2]; scb = sc[hp % 2]
                            kb.dma("sp", f"wq{hp % 2}", ws.t[:], wqv[:, :, hp * 128:(hp + 1) * 128], writes=[ws])
                            op("pool", lambda e: e.tensor_copy(out=wb.t[:], in_=ws.t[:]), reads=[ws], writes=[wb])
                            for blk in range(NQB):
                                bank = pb[blk % 4]
                                for c in range(8):
                                    op("pe", lambda e: e.matmul(out=bank.t[:, 0:QBW], lhsT=wb.t[:, c, :], rhs=hT.t[:, c, blk * QBW:(blk + 1) * QBW],
                                                                start=(c == 0), stop=(c == 7)), reads=[wb, hT], writes=[bank])
                                op("act", lambda e: e.activation(out=qt_.t[:, blk * QBW:(blk + 1) * QBW], in_=bank.t[:, 0:QBW], func=AF.Copy),
                                   reads=[bank], writes=[qt_])
                            for blk in range(NQB):
                                bank = pb[4 + blk % 4]
                                for tt in range(TPB):
                                    t = blk * TPB + tt
                                    op("pe", lambda e: e.matmul(out=bank.t[:, tt * 128:(tt + 1) * 128], lhsT=qt_.t[:, t * 128:(t + 1) * 128], rhs=keysT.t[:, hp, :],
                                                                start=True, stop=True), reads=[qt_, keysT], writes=[bank])
                                op("act", lambda e: e.activation(out=scb.t[:, blk * QBW:(blk + 1) * QBW], in_=bank.t[:, 0:QBW], func=AF.Copy),
                                   reads=[bank], writes=[scb])
                            TG = 4
                            for t0_ in range(0, NT, TG):
                                tl = list(range(t0_, min(NT, t0_ + TG)))
                                sv = {t: scb.t[:, t * 128:(t + 1) * 128] for t in tl}
                                for t in tl:
                                    op("dve", lambda e: e.max(out=top_s.t[:, t, hp, 0:8], in_=sv[t]), reads=[scb], writes=[tsb[t]])
                                for t in tl:
                                    op("dve", lambda e: e.tensor_scalar(out=mrs[t % TG].t[:], in0=sv[t], scalar1=top_s.t[:, t, hp, 7:8], scalar2=None, op0=ALU.is_ge),
                                       reads=[scb], sreads=[tsb[t]], writes=[mrs[t % TG]])
                                for t in tl:
                                    op("dve", lambda e: e.scalar_tensor_tensor(out=mrs[t % TG].t[:], in0=mrs[t % TG].t[:], scalar=-1e30, in1=sv[t], op0=ALU.mult, op1=ALU.add),
                                       reads=[scb, mrs[t % TG]], writes=[mrs[t % TG]])
                                for t in tl:
                                    op("dve", lambda e: e.max(out=top_s.t[:, t, hp, 8:16], in_=mrs[t % TG].t[:]), reads=[mrs[t % TG]], writes=[tsb[t]])
                                for t in tl:
                                    op("dve", lambda e: e.max_index(out=top_i.t[:, t, hp, 0:8], in_max=top_s.t[:, t, hp, 0:8], in_values=sv[t]),
                                       reads=[scb, tsb[t]], writes=[tib[t]])
                                for t in tl:
                                    op("dve", lambda e: e.max_index(out=top_i.t[:, t, hp, 8:16], in_max=top_s.t[:, t, hp, 8:16], in_values=sv[t]),
                                       reads=[scb, tsb[t]], writes=[tib[t]])
                        for t in range(NT):
                            top_s.w = tsb[t].w if top_s.w is None or (tsb[t].w and tsb[t].w[1] > top_s.w[1]) else top_s.w
                            top_i.w = tib[t].w if top_i.w is None or (tib[t].w and tib[t].w[1] > top_i.w[1]) else top_i.w
                        kb.barrier()
                    with ExitStack() as sg:
                        tif = kb.sb("tif", [128, 16, 16], F32, sg)
                        cand = kb.sb("cand", [128, 8, 16, 16], F32, sg)
                        mr2 = kb.sb("mr2", [128, 256], F32, sg)
                        bs = kb.sb("bs", [128, 8, 16], F32, sg)
                        bp = kb.sb("bp", [128, 8, 16], U32, sg)
                        ab_i = kb.sb("ab_i", [128, 2, 128], U32, sg)
                        ab_f = kb.sb("ab_f", [128, 2, 8, 16], F32, sg)
                        oh = kb.sb("oh", [128, 8, 16, 16], F32, sg)
                        sel = kb.sb("sel", [128, 2, 128], F32, sg)
                        idxf = kb.sb("idxf", [128, 128], F32, sg)
                        idx = [kb.sb(f"idx{i}", [128, 128], I32, sg) for i in range(2)]
                        gate = [kb.sb(f"gate{i}", [128, 8, 16], F32, sg) for i in range(2)]
                        gsum = kb.sb("gsum", [128, 8], F32, sg)
                        act = [kb.sb(f"act{i}", [128, 128], F32, sg) for i in range(2)]
                        ga = [kb.sb(f"ga{i}", [128, 128], F32, sg) for i in range(2)]
                        gb_ = [kb.sb(f"gb{i}", [128, 2048], BF16, sg) for i in range(NB_G)]
                        prod = [kb.sb(f"prod{i}", [128, D], BF16, sg) for i in range(2)]
                        xbf = [kb.sb(f"xbf{i}", [128, D], BF16, sg) for i in range(2)]
                        y = kb.sb("y", [128, D], F32, sg)
                        junkb = kb.sb("junkb", [128, D], BF16, sg)
                        junkd = Buf("junkd", junkb.t)
                        g2 = kb.sb("g2", [128, D], F32, sg)
                        b2 = kb.sb("b2", [128, D], F32, sg)
                        st = kb.sb("st", [128, 2, 6], F32, sg)
                        mv = kb.sb("mv", [128, 2], F32, sg)
                        r2 = kb.sb("r2", [128, 2], F32, sg)
                        kb.dma("sp", "c2", g2.t[:], bc(lng_h[l, 2:3, :], [128, D]), writes=[g2])
                        kb.dma("sp", "c2", b2.t[:], bc(lng_h[l, 3:4, :], [128, D]), writes=[b2])

                        def p2_ops(t):
                            ix = idx[t % 2]; gt_ = gate[t % 2]
                            ts4 = top_s.t[:, t, :, :].rearrange("p (h two) k -> p h two k", two=2)
                            tif4 = tif.t[:].rearrange("p (h two) k -> p h two k", two=2)
                            T = []
                            T.append(lambda: op("dve", lambda e: e.tensor_copy(out=tif.t[:], in_=top_i.t[:, t, :, :]), reads=[top_i], writes=[tif]))
                            T.append(lambda: op("dve", lambda e: e.tensor_tensor(out=cand.t[:], in0=bc(ts4[:, :, 0, :].unsqueeze(3), [128, 8, 16, 16]),
                                                                                 in1=bc(ts4[:, :, 1, :].unsqueeze(2), [128, 8, 16, 16]), op=ALU.add),
                                                reads=[top_s], writes=[cand]))
                            for hh in range(8):
                                def f(hh=hh):
                                    cv_ = cand.t[:, hh, :, :].rearrange("p a b -> p (a b)")
                                    op("dve", lambda e: e.max(out=bs.t[:, hh, 0:8], in_=cv_), reads=[cand], writes=[bs])
                                    op("dve", lambda e: e.tensor_scalar(out=mr2.t[:], in0=cv_, scalar1=bs.t[:, hh, 7:8], scalar2=None, op0=ALU.is_ge),
                                       reads=[cand], sreads=[bs], writes=[mr2])
                                    op("dve", lambda e: e.scalar_tensor_tensor(out=mr2.t[:], in0=mr2.t[:], scalar=-1e30, in1=cv_, op0=ALU.mult, op1=ALU.add),
                                       reads=[cand, mr2], writes=[mr2])
                                    op("dve", lambda e: e.max(out=bs.t[:, hh, 8:16], in_=mr2.t[:]), reads=[mr2], writes=[bs])
                                    op("dve", lambda e: e.max_index(out=bp.t[:, hh, 0:8], in_max=bs.t[:, hh, 0:8], in_values=cv_), reads=[cand, bs], writes=[bp])
                                    op("dve", lambda e: e.max_index(out=bp.t[:, hh, 8:16], in_max=bs.t[:, hh, 8:16], in_values=cv_), reads=[cand, bs], writes=[bp])
                                T.append(f)
                            bpf = bp.t[:].rearrange("p h k -> p (h k)")
                            T.append(lambda: op("dve", lambda e: e.tensor_single_scalar(out=ab_i.t[:, 0, :], in_=bpf, scalar=4, op=ALU.logical_shift_right), reads=[bp], writes=[ab_i]))
                            T.append(lambda: op("dve", lambda e: e.tensor_single_scalar(out=ab_i.t[:, 1, :], in_=bpf, scalar=15, op=ALU.bitwise_and), reads=[bp], writes=[ab_i]))
                            T.append(lambda: op("dve", lambda e: e.tensor_copy(out=ab_f.t[:].rearrange("p a h k -> p a (h k)"), in_=ab_i.t[:]), reads=[ab_i], writes=[ab_f]))
                            for w in range(2):
                                T.append(lambda w=w: op("dve", lambda e: e.tensor_tensor(out=oh.t[:], in0=bc(ab_f.t[:, w, :, :].unsqueeze(3), [128, 8, 16, 16]),
                                                                                         in1=bc(iota16.t[:].unsqueeze(1).unsqueeze(1), [128, 8, 16, 16]), op=ALU.is_equal),
                                                        reads=[ab_f, iota16], writes=[oh]))
                                T.append(lambda w=w: op("dve", lambda e: e.tensor_tensor(out=oh.t[:], in0=oh.t[:], in1=bc(tif4[:, :, w, :].unsqueeze(2), [128, 8, 16, 16]), op=ALU.mult),
                                                        reads=[oh, tif], writes=[oh]))
                                T.append(lambda w=w: op("dve", lambda e: e.tensor_reduce(out=sel.t[:, w, :], in_=oh.t[:].rearrange("p h k a -> p (h k) a"), axis=AX.X, op=ALU.add),
                                                        reads=[oh], writes=[sel]))
                            T.append(lambda: op("dve", lambda e: e.tensor_scalar(out=idxf.t[:], in0=sel.t[:, 0, :], scalar1=128.0, scalar2=float(l * NEXP), op0=ALU.mult, op1=ALU.add),
                                                reads=[sel], writes=[idxf]))
                            T.append(lambda: op("dve", lambda e: e.tensor_tensor(out=idxf.t[:], in0=idxf.t[:], in1=sel.t[:, 1, :], op=ALU.add), reads=[idxf, sel], writes=[idxf]))
                            T.append(lambda: op("dve", lambda e: e.tensor_copy(out=ix.t[:], in_=idxf.t[:]), reads=[idxf], writes=[ix]))
                            T.append(lambda: op("dve", lambda e: e.tensor_tensor(out=gt_.t[:], in0=bs.t[:], in1=bc(bs.t[:, :, 0:1], [128, 8, 16]), op=ALU.subtract), reads=[bs], writes=[gt_]))
                            T.append(lambda: op("act", lambda e: e.activation(out=gt_.t[:], in_=gt_.t[:], func=AF.Exp), reads=[gt_], writes=[gt_]))
                            T.append(lambda: op("dve", lambda e: e.tensor_reduce(out=gsum.t[:], in_=gt_.t[:], axis=AX.X, op=ALU.add), reads=[gt_], writes=[gsum]))
                            T.append(lambda: op("dve", lambda e: e.reciprocal(out=gsum.t[:], in_=gsum.t[:]), reads=[gsum], writes=[gsum]))
                            T.append(lambda: op("dve", lambda e: e.tensor_tensor(out=gt_.t[:], in0=gt_.t[:], in1=bc(gsum.t[:].unsqueeze(2), [128, 8, 16]), op=ALU.mult),
                                                reads=[gt_, gsum], writes=[gt_]))
                            if t == 0:
                                T.append(lambda: (dump("top_s", top_s.t[:, 0, :, :], [128, 16, 16], F32, [top_s]),
                                                  dump("idx", ix.t[:], [128, 128], I32, [ix]),
                                                  dump("gate", gt_.t[:], [128, 8, 16], F32, [gt_])))
                            return T

                        for th in p2_ops(0):
                            th()
                        gcount = 0; pcount = 0; dcount = 0
                        Dg = [kb.sb(f"Dg{i}", [128, 128], BF16, sg) for i in range(6)]
                        EB = 4
                        NBATCH = 128 // EB
                        for t in range(NT):
                            ix = idx[t % 2]; gt_ = gate[t % 2]; at = act[t % 2]; gat = ga[t % 2]; xb = xbf[t % 2]
                            nxt = p2_ops(t + 1) if t + 1 < NT else []
                            per = -(-len(nxt) // NBATCH) if nxt else 0
                            op("act", lambda e: e.activation(out=xb.t[:], in_=h.t[:, t, :], func=AF.Copy), reads=[ht[t]], writes=[xb])
                            slots = {}
                            for b in range(NBATCH + 2):
                                if 1 <= b <= NBATCH:
                                    sl = slice((b - 1) * EB, b * EB)
                                    op("act", lambda e: e.activation(out=gat.t[:, sl], in_=at.t[:, sl], func=AF.Gelu), reads=[at], writes=[gat])
                                if b >= 2:
                                    sl = slice((b - 2) * EB, (b - 1) * EB)
                                    op("dve", lambda e: e.tensor_tensor(out=gat.t[:, sl], in0=gat.t[:, sl], in1=gt_.t[:].rearrange("p h k -> p (h k)")[:, sl], op=ALU.mult),
                                       reads=[gat, gt_], writes=[gat])
                                    for ei in range((b - 2) * EB, (b - 1) * EB):
                                        gbuf = gb_[slots[ei]]
                                        dg = Dg[dcount % len(Dg)]; dcount += 1
                                        op("act", lambda e: e.activation(out=dg.t[:], in_=identb.t[:], func=AF.Copy, scale=gat.t[:, ei:ei + 1]),
                                           reads=[identb], sreads=[gat], writes=[dg])
                                        for half in range(2):
                                            yb = pb[2 * (t % 2) + half]
                                            op("pe", lambda e: e.matmul(out=yb.t[:, :], lhsT=dg.t[:], rhs=gbuf.t[:, D + half * 512:D + (half + 1) * 512],
                                                                        start=(ei == 0), stop=(ei == 127)), reads=[dg, gbuf], writes=[yb])
                                if b < NBATCH:
                                    for ei in range(b * EB, (b + 1) * EB):
                                        s_ = gcount % NB_G; gcount += 1; slots[ei] = s_
                                        gbuf = gb_[s_]
                                        kb.dma("pool", f"g{s_}", gbuf.t[:, :], uv_h[:, :], reads=[ix, scr[l]], writes=[gbuf], indirect=ix.t[:, ei:ei + 1])
                                        if ei % 2 == 0:
                                            op("dve", lambda e: e.scalar_tensor_tensor(out=junkd.t[:], in0=gbuf.t[:, 0:D], scalar=1.0, in1=xb.t[:], op0=ALU.mult, op1=ALU.mult,
                                                                                       accum_out=at.t[:, ei:ei + 1]), reads=[gbuf, xb], writes=[junkd, at])
                                        else:
                                            pr_ = prod[pcount % 2]; pcount += 1
                                            op("dve", lambda e: e.tensor_tensor(out=pr_.t[:], in0=gbuf.t[:, 0:D], in1=xb.t[:], op=ALU.mult), reads=[gbuf, xb], writes=[pr_])
                                            op("act", lambda e: e.activation(out=junkb.t[:], in_=pr_.t[:], func=AF.Copy, accum_out=at.t[:, ei:ei + 1]),
                                               reads=[pr_], writes=[junkb, at])
                                for _ in range(per):
                                    if nxt:
                                        nxt.pop(0)()
                            while nxt:
                                nxt.pop(0)()
                            if t == 0:
                                dump("act", at.t[:], [128, 128], F32, [at])
                            for half in range(2):
                                yb = pb[2 * (t % 2) + half]
                                op("dve", lambda e: e.scalar_tensor_tensor(out=y.t[:, half * 512:(half + 1) * 512], in0=h.t[:, t, half * 512:(half + 1) * 512], scalar=ALPHA,
                                                                           in1=yb.t[:, :], op0=ALU.mult, op1=ALU.add), reads=[ht[t], yb], writes=[y])
                            layer_norm_tile(t, y.t[:], g2, b2, y, st, mv, r2)
                        kb.barrier()
        ov = out_h.ap().rearrange("(n p) d -> p n d", p=128)
        for t in range(NT):
            kb.dma("sp", "out", ov[:, t, :], h.t[:, t, :], reads=[ht[t]])
        kb.barrier()
    return nc


def core_inputs(S, L, x_b, P, consts):
    m = {"x": np.ascontiguousarray(x_b.reshape(S, D))}
    m.update(P)
    m.update({"c_" + k: v for k, v in consts.items()})
    return m


def pack_params(inputs, L):
    f = lambda a: np.ascontiguousarray(np.asarray(a, dtype=np.float32))
    P = {
        "w_in": f(inputs["w_in"][:L]),
        "lam4": f(np.stack([inputs["lam_q1"][:L], inputs["lam_k1"][:L], inputs["lam_q2"][:L], inputs["lam_k2"][:L]], axis=1)),
        "subln_g": f(inputs["subln_g"][:L]),
        "w_o": f(inputs["w_o"][:L]),
        "ln_gb": f(np.stack([inputs["ln1_g"][:L], inputs["ln1_b"][:L], inputs["ln2_g"][:L], inputs["ln2_b"][:L]], axis=1)),
        "w_query": f(inputs["w_query"][:L]),
        "sub_keys": f(np.asarray(inputs["sub_keys"][:L]).reshape(L, 16, 128, 128)),
        "expert_u": f(np.asarray(inputs["expert_u"][:L]).reshape(L * NEXP, D)),
        "expert_v": f(np.asarray(inputs["expert_v"][:L]).reshape(L * NEXP, D)),
    }
    return P


def kernel(**inputs):
    x = np.asarray(inputs["x"], dtype=np.float32)
    B, S, _ = x.shape
    L = DEPTH
    P = pack_params(inputs, L)
    consts = host_consts(S)
    nc = build_program(S, L)
    in_maps = [core_inputs(S, L, x[b], P, consts) for b in range(B)]
    res = run_bass_kernel_spmd(nc, in_maps, core_ids=list(range(B)))
    out = np.stack([np.asarray(r["out"]).reshape(S, D) for r in res.results], axis=0)
    return out.astype(np.float32)
```
